# Optimizing a Trainium2 kernel written in Bass

```python
import jax, jax.numpy as jnp
from jax import lax
import numpy as np

D_MODEL = 2048
BATCH = 4
SEQ = 4096
DEPTH = 4

N_MEM = 256
GRID_W = 64
HEAD_DIM = 128
MIX_WIDTH = D_MODEL
MEM_HEADS = 4
MEM_WIDTH = MEM_HEADS * HEAD_DIM
TOK_WIDTH = MIX_WIDTH - MEM_WIDTH
CHUNK = 128
A_GROUPS = TOK_WIDTH // HEAD_DIM
A_GROUP_DIM = HEAD_DIM
Q_HEADS = TOK_WIDTH // HEAD_DIM
KV_HEADS = 4
Q_PER_KV = Q_HEADS // KV_HEADS
KV_WIDTH = KV_HEADS * HEAD_DIM
Q_BLOCK = 128
ROPE_THETA = 10000.0
ROPE_PAIRS = HEAD_DIM // 4
D_FF = ((8 * D_MODEL // 3 + 255) // 256) * 256
N_MIXERS = 2
N_A = (DEPTH + 1) // 2
N_B = DEPTH // 2
EPS = 1e-6

kernel_name = "hybrid_gmlp_axial_gqa_memory_encoder"


def rms_norm(x, g):
    xf = x.astype(jnp.float32)
    y = xf * lax.rsqrt(jnp.mean(xf * xf, axis=-1, keepdims=True) + EPS)
    return (y * g.astype(jnp.float32)).astype(x.dtype)


def axial_rope_tables(seq_len):
    n_rows = seq_len // GRID_W
    rows = jnp.broadcast_to(jnp.arange(n_rows)[:, None], (n_rows, GRID_W)).reshape(seq_len)
    cols = jnp.broadcast_to(jnp.arange(GRID_W)[None, :], (n_rows, GRID_W)).reshape(seq_len)
    freqs = ROPE_THETA ** (-jnp.arange(ROPE_PAIRS, dtype=jnp.float32) / ROPE_PAIRS)
    ang_r = rows.astype(jnp.float32)[:, None] * freqs
    ang_c = cols.astype(jnp.float32)[:, None] * freqs
    ang = jnp.concatenate([ang_r, ang_r, ang_c, ang_c], axis=-1)
    return jnp.cos(ang), jnp.sin(ang)


def apply_axial_rope(x, cos, sin):
    shape = (1, cos.shape[0]) + (1,) * (x.ndim - 3) + (HEAD_DIM,)
    c, s = cos.reshape(shape), sin.reshape(shape)
    xf = x.astype(jnp.float32)
    xs = xf.reshape(x.shape[:-1] + (2, 2, ROPE_PAIRS))
    rot = jnp.stack([-xs[..., 1, :], xs[..., 0, :]], axis=-2).reshape(x.shape)
    return (xf * c + rot * s).astype(x.dtype)


def chunked_spatial_gating(uv, g_v, w_s, b_s):
    B, S, _ = uv.shape
    uv = jax.nn.gelu(uv, approximate=False)
    u, v = jnp.split(uv, 2, axis=-1)
    v = rms_norm(v, g_v)
    v = v.reshape(B, S // CHUNK, CHUNK, A_GROUPS, A_GROUP_DIM)
    s = jnp.einsum('gts,bcsgd->bctgd', w_s, v) + b_s.T[None, None, :, :, None]
    return u * s.reshape(B, S, TOK_WIDTH)


def gqa_axial_attention(qkv, g_q, g_k, cos, sin):
    B, S, _ = qkv.shape
    q, k, v = jnp.split(qkv, [TOK_WIDTH, TOK_WIDTH + KV_WIDTH], axis=-1)
    q = q.reshape(B, S, KV_HEADS, Q_PER_KV, HEAD_DIM)
    k = k.reshape(B, S, KV_HEADS, HEAD_DIM)
    v = v.reshape(B, S, KV_HEADS, HEAD_DIM)
    q = apply_axial_rope(rms_norm(q, g_q), cos, sin)
    k = apply_axial_rope(rms_norm(k, g_k), cos, sin)
    scale = HEAD_DIM ** -0.5
    n_blk = S // Q_BLOCK
    qb = q.reshape(B, n_blk, Q_BLOCK, KV_HEADS, Q_PER_KV, HEAD_DIM).transpose(1, 0, 2, 3, 4, 5)

    def block(qi):
        s = jnp.einsum('bqhgd,bkhd->bhgqk', qi, k).astype(jnp.float32) * scale
        p = jax.nn.softmax(s, axis=-1).astype(v.dtype)
        return jnp.einsum('bhgqk,bkhd->bqhgd', p, v)

    o = lax.map(block, qb)
    return o.transpose(1, 0, 2, 3, 4, 5).reshape(B, S, TOK_WIDTH)


def memory_attention(q_mem, mem, g_mem, w_kv, g_mq, g_mk):
    B, S, _ = q_mem.shape
    kv = rms_norm(mem, g_mem) @ w_kv
    k, v = jnp.split(kv, 2, axis=-1)
    k = rms_norm(k.reshape(B, -1, MEM_HEADS, HEAD_DIM), g_mk)
    v = v.reshape(B, -1, MEM_HEADS, HEAD_DIM)
    q = rms_norm(q_mem.reshape(B, S, MEM_HEADS, HEAD_DIM), g_mq)
    s = jnp.einsum('bshd,bmhd->bhsm', q, k).astype(jnp.float32) * (HEAD_DIM ** -0.5)
    p = jax.nn.softmax(s, axis=-1).astype(v.dtype)
    return jnp.einsum('bhsm,bmhd->bshd', p, v).reshape(B, S, MEM_WIDTH)


def swiglu(h, w_gate_up, w_down):
    gate, up = jnp.split(h @ w_gate_up, 2, axis=-1)
    return (jax.nn.silu(gate) * up) @ w_down


def setup_inputs(seed: int = 0) -> dict:
    key = jax.random.key(seed)
    ks = jax.random.split(key, 20)
    f32 = jnp.float32

    def nrm(k, shape, fan_in):
        return jax.random.normal(k, shape, f32) * (fan_in ** -0.5)

    def gain(k, shape):
        return 1.0 + 0.02 * jax.random.normal(k, shape, f32)

    return {
        "x": jax.random.normal(ks[0], (BATCH, SEQ, D_MODEL), f32),
        "mem": jax.random.normal(ks[1], (BATCH, N_MEM, D_MODEL), f32),
        "g_mix": gain(ks[2], (DEPTH, D_MODEL)),
        "g_ffn": gain(ks[3], (DEPTH, D_MODEL)),
        "w_in_a": nrm(ks[4], (N_A, D_MODEL, 2 * TOK_WIDTH + MEM_WIDTH), D_MODEL),
        "g_v_a": gain(ks[5], (N_A, TOK_WIDTH)),
        "w_spatial": nrm(ks[6], (N_A, A_GROUPS, CHUNK, CHUNK), CHUNK),
        "b_spatial": 0.02 * jax.random.normal(ks[7], (N_A, A_GROUPS, CHUNK), f32),
        "w_in_b": nrm(ks[8], (N_B, D_MODEL, TOK_WIDTH + 2 * KV_WIDTH + MEM_WIDTH), D_MODEL),
        "g_q_b": gain(ks[9], (N_B, HEAD_DIM)),
        "g_k_b": gain(ks[10], (N_B, HEAD_DIM)),
        "g_mem": gain(ks[11], (DEPTH, D_MODEL)),
        "w_mem_kv": nrm(ks[12], (DEPTH, D_MODEL, 2 * MEM_WIDTH), D_MODEL),
        "g_mq": gain(ks[13], (DEPTH, HEAD_DIM)),
        "g_mk": gain(ks[14], (DEPTH, HEAD_DIM)),
        "w_out": nrm(ks[15], (DEPTH, MIX_WIDTH, D_MODEL), MIX_WIDTH),
        "w_gate_up": nrm(ks[16], (DEPTH, D_MODEL, 2 * D_FF), D_MODEL),
        "w_down": nrm(ks[17], (DEPTH, D_FF, D_MODEL), D_FF),
    }


def reference(x, mem, g_mix, g_ffn, w_in_a, g_v_a, w_spatial, b_spatial, w_in_b, g_q_b, g_k_b,
              g_mem, w_mem_kv, g_mq, g_mk, w_out, w_gate_up, w_down):
    S = x.shape[1]
    cos, sin = axial_rope_tables(S)
    for l in range(DEPTH):
        h = rms_norm(x, g_mix[l])
        if l % N_MIXERS == 0:
            ia = l // N_MIXERS
            z = h @ w_in_a[ia]
            tok_in, q_mem = jnp.split(z, [2 * TOK_WIDTH], axis=-1)
            tok_out = chunked_spatial_gating(tok_in, g_v_a[ia], w_spatial[ia], b_spatial[ia])
        else:
            ib = l // N_MIXERS
            z = h @ w_in_b[ib]
            tok_in, q_mem = jnp.split(z, [TOK_WIDTH + 2 * KV_WIDTH], axis=-1)
            tok_out = gqa_axial_attention(tok_in, g_q_b[ib], g_k_b[ib], cos, sin)
        mem_out = memory_attention(q_mem, mem, g_mem[l], w_mem_kv[l], g_mq[l], g_mk[l])
        x = x + jnp.concatenate([tok_out, mem_out], axis=-1) @ w_out[l]
        x = x + swiglu(rms_norm(x, g_ffn[l]), w_gate_up[l], w_down[l])
    return x
```

```python
import numpy as np
import concourse.bass as bass
import concourse.mybir as mybir
from concourse.bass_utils import run_bass_kernel_spmd

F32 = mybir.dt.float32
BF16 = mybir.dt.bfloat16
AF = mybir.ActivationFunctionType
ALU = mybir.AluOpType
AX = mybir.AxisListType

D = 2048
KC = 16
T = 512
NBLK = 4
TOK = 2048
SEQ = 4096
DFF = 5632
FC = 44
TW = 1536
HD = 128
EPS = 1e-6
N_MEM = 256
SCALE = HD ** -0.5

C_GMIX, C_GFFN, C_GMEM, C_GMQ, C_GMK, C_GQ, C_GK, C_GV, NCOLS = 0, 64, 128, 192, 196, 200, 202, 204, 228

FUSED = True


class Tok:
    __slots__ = ("sem", "v", "entry", "small")

    def __init__(self, sem, v, entry=None, small=False):
        self.sem, self.v, self.entry, self.small = sem, v, entry, small


class Sched:
    ENG = ("pe", "act", "dve", "pool", "sp")
    TRACE = False

    def __init__(self):
        self.lists = {e: [] for e in self.ENG}
        self.cnt = {}
        self.waited = {}
        self.last_w = {}
        self.readers = {}
        self.sem_names = list(self.ENG)
        self.pe_labels = []
        self.pending = {e: [] for e in self.ENG}

    def new_sem(self, name):
        assert name not in self.sem_names
        self.sem_names.append(name)
        return name

    def _resolve(self, tok):
        if tok.v is not None:
            return
        pend = self.pending[tok.sem]
        i = pend.index(tok)
        tok.entry[3] = 1
        v = self.cnt.get(tok.sem, 0) + 1
        self.cnt[tok.sem] = v
        for t in pend[:i + 1]:
            t.v = v
        del pend[:i + 1]

    def _collect(self, reads, writes):
        toks = []
        for k in reads:
            t = self.last_w.get(k)
            if t is not None:
                toks.append(t)
        for k in writes:
            t = self.last_w.get(k)
            if t is not None:
                toks.append(t)
            toks.extend(self.readers.get(k, ()))
        return toks

    def _emit_waits(self, eng, toks, skip_same=True):
        deps = {}
        for t in toks:
            if skip_same and t.sem == eng:
                if eng == "pe":
                    continue
                if t.v is not None and self.cnt.get(eng, 0) - t.v >= 4:
                    continue
            self._resolve(t)
            if deps.get(t.sem, 0) < t.v:
                deps[t.sem] = t.v
        for s_, v in deps.items():
            if self.waited.get((eng, s_), 0) >= v:
                continue
            self.waited[(eng, s_)] = v
            self.lists[eng].append(("wait", s_, v))

    def op(self, eng, fn, reads=(), writes=(), sem=None, inc=None, lazy=False, small=False):
        semname = sem or eng
        if inc is None:
            inc = 1 if sem is None else 16
        self._emit_waits(eng, self._collect(reads, writes))
        if lazy:
            entry = ["op", fn, semname, 0]
            tok = Tok(semname, None, entry)
            self.pending[semname].append(tok)
        else:
            v = self.cnt.get(semname, 0) + inc
            self.cnt[semname] = v
            entry = ["op", fn, semname, inc]
            tok = Tok(semname, v, small=small)
            if semname in self.pending:
                for t in self.pending[semname]:
                    t.v = v
                self.pending[semname] = []
        self.lists[eng].append(entry)
        if eng == "pe" and Sched.TRACE:
            import sys as _sys
            f = _sys._getframe(1)
            names = []
            while f is not None and len(names) < 8:
                names.append(f"{f.f_code.co_name}:{f.f_lineno}")
                f = f.f_back
            self.pe_labels.append(names)
        for k in writes:
            self.last_w[k] = tok
            self.readers[k] = []
        for k in reads:
            self.readers.setdefault(k, []).append(tok)
        return tok

    def wait_all(self, eng, keys):
        toks = []
        for k in keys:
            t = self.last_w.get(k)
            if t is not None:
                toks.append(t)
            toks.extend(self.readers.get(k, ()))
        self._emit_waits(eng, toks, skip_same=False)


class Buf:
    def __init__(self, ap, atoms):
        self.ap = ap
        self.atoms = list(atoms)


def _weight_sets(phases):
    Ls, As, Bs = [], [], []
    for ph in phases:
        l_, a_, b_ = {0: ([0], [0], [0]), 1: ([1, 2], [1], [0, 1]), 2: ([3], [], [1])}[ph]
        Ls += [x for x in l_ if x not in Ls]
        As += [x for x in a_ if x not in As]
        Bs += [x for x in b_ if x not in Bs]
    return sorted(Ls), sorted(As), sorted(Bs)


def _host_consts():
    ident = np.eye(128, dtype=np.float32)
    rot = np.zeros((128, 128), np.float32)
    for a in range(2):
        for p in range(32):
            rot[a * 64 + 32 + p, a * 64 + p] = -1.0
            rot[a * 64 + p, a * 64 + 32 + p] = 1.0
    return ident, rot


def _rope_tables():
    n_rows = SEQ // 64
    pos = np.arange(SEQ)
    rows = (pos // 64).astype(np.float32)
    cols = (pos % 64).astype(np.float32)
    freqs = (np.float32(10000.0) ** (-np.arange(32, dtype=np.float32) / np.float32(32))).astype(np.float32)
    ang_r = rows[:, None] * freqs
    ang_c = cols[:, None] * freqs
    ang = np.concatenate([ang_r, ang_r, ang_c, ang_c], axis=-1).astype(np.float32)
    return np.ascontiguousarray(np.cos(ang).T.astype(np.float32)), np.ascontiguousarray(np.sin(ang).T.astype(np.float32))


def build(phases, fused):
    nc = bass.Bass("TRN2", target_bir_lowering=False)
    S = Sched()

    def din(name, shape, dt=F32):
        return nc.dram_tensor(name, list(shape), dt, kind="ExternalInput").ap()

    def dout(name, shape, dt=F32):
        return nc.dram_tensor(name, list(shape), dt, kind="ExternalOutput").ap()

    def dint(name, shape, dt=F32):
        return nc.dram_tensor(name, list(shape), dt).ap()

    first_phase, last_phase = phases[0], phases[-1]
    x_in = din("x_in", [NBLK, 128, KC * T]) if 0 in phases else None
    y_out = dout("y_out", [NBLK, 128, KC * T]) if 2 in phases else None
    mem_in = din("mem", [N_MEM, D])
    cols_in = din("cols", [128, NCOLS])
    ident_in = din("ident", [128, 128])
    rot_in = din("rot", [128, 128])
    cos_in = din("cosT", [128, TOK])
    sin_in = din("sinT", [128, TOK])
    Ls, As, Bs = _weight_sets(phases)
    Lmap = {l: i for i, l in enumerate(Ls)}
    Amap = {a: i for i, a in enumerate(As)}
    Bmap = {b: i for i, b in enumerate(Bs)}

    class _Idx:
        def __init__(self, ap, m):
            self.ap, self.m = ap, m

        def __getitem__(self, i):
            return self.ap[self.m[i]]

    w_in_a = _Idx(din("w_in_a", [max(len(As), 1), 14, 128, KC * 256]), Amap)
    w_in_b = _Idx(din("w_in_b", [max(len(Bs), 1), 12, 128, KC * 256]), Bmap)
    w_spT = din("w_spT", [2, 128, TW])
    bsp_in = din("bsp", [2, 128, TW])
    w_mem_kv = _Idx(din("w_mem_kv", [len(Ls), 4, 128, KC * 256]), Lmap)
    w_out = _Idx(din("w_out", [len(Ls), 8, 128, KC * 256]), Lmap)
    w_gu = _Idx(din("w_gate_up", [len(Ls), 44, 128, KC * 256]), Lmap)
    w_dn = _Idx(din("w_down", [len(Ls), 8, 128, FC * 256]), Lmap)

    if fused:
        xpark = dint("xpark", [NBLK, 128, KC * T])
        xpark_r = xpark_w = xpark
        kT_loc = [dint(f"kT_loc{e}", [512, TOK], BF16) for e in range(2)]
        v_loc = [dint(f"v_loc{e}", [TOK, 512], BF16) for e in range(2)]
        kT_all = [dint(f"kT_all{e}", [1024, TOK], BF16) for e in range(2)]
        v_all = [dint(f"v_all{e}", [SEQ, 512], BF16) for e in range(2)]
    else:
        xpark_r = din("xpark_in", [NBLK, 128, KC * T]) if first_phase > 0 else None
        xpark_w = dout("xpark_out", [NBLK, 128, KC * T]) if last_phase < 2 else None
        kT_loc = [None, None]
        v_loc = [None, None]
        kT_all = [None, None]
        v_all = [None, None]
        if 0 in phases:
            kT_loc[0] = dout("kT_loc0", [512, TOK], BF16)
            v_loc[0] = dout("v_loc0", [TOK, 512], BF16)
        if 1 in phases:
            kT_all[0] = din("kT_all0", [1024, TOK], BF16)
            v_all[0] = din("v_all0", [SEQ, 512], BF16)
            kT_loc[1] = dout("kT_loc1", [512, TOK], BF16)
            v_loc[1] = dout("v_loc1", [TOK, 512], BF16)
        if 2 in phases:
            kT_all[1] = din("kT_all1", [1024, TOK], BF16)
            v_all[1] = din("v_all1", [SEQ, 512], BF16)

    sb = {}
    ctxs = []

    def salloc(name, shape, dt):
        cm = nc.sbuf_tensor(name, list(shape), dt)
        t = cm.__enter__()
        ctxs.append(cm)
        sb[name] = t
        return t

    xT_t = salloc("xT", [128, KC * T], F32)
    hT_t = salloc("hT", [128, KC * T], BF16)
    NW = 5
    W_t = salloc("W", [128, NW * 4096], BF16)
    KV_t = salloc("KV", [128, 2 * 8192], BF16)
    R_t = salloc("R", [128, FC * T], BF16)
    rstd_t = salloc("rstdb", [128, T], F32)
    sq_t = salloc("sq", [128, 2 * T], BF16)
    silu_t = salloc("silu", [128, 2 * T], F32)
    recip_t = salloc("recip", [128, T], F32)
    lnt_t = salloc("lnt", [128, T], F32)
    memkv_t = salloc("memkv", [128, 2 * 2048], BF16)
    wspf_t = salloc("wspf", [128, TW], F32)
    bsp_t = salloc("bspt", [128, TW], F32)
    cols_t = salloc("colst", [128, NCOLS], F32)
    colsS_t = salloc("colsS", [128, NCOLS], F32)
    ones_t = salloc("ones", [128, 128], BF16)
    ident_t = salloc("identt", [128, 128], F32)
    rot_t = salloc("rott", [128, 128], BF16)
    small_t = salloc("small", [128, 64], F32)
    PS_cm = nc.psum_tensor("ps", [128, 8 * 512], F32)
    PS_t = PS_cm.__enter__()
    ctxs.append(PS_cm)

    xT = [Buf(xT_t[:, k * T:(k + 1) * T], [("xT", k)]) for k in range(KC)]
    hT = [Buf(hT_t[:, k * T:(k + 1) * T], [("hT", k)]) for k in range(KC)]
    Wb = [Buf(W_t[:, s * 4096:(s + 1) * 4096], [("W", s)]) for s in range(NW)]
    KTb = [Buf(KV_t[:, s * 8192:s * 8192 + 4096], [("KT", s)]) for s in range(2)]
    Vb = [Buf(KV_t[:, s * 8192 + 4096:(s + 1) * 8192], [("V", s)]) for s in range(2)]
    Ratom = lambda a0, n=1: [("R", a) for a in range(a0, a0 + n)]
    actT = [Buf(R_t[:, j * T:(j + 1) * T], Ratom(j)) for j in range(FC)]
    catT = actT[:16]
    vn = [Buf(R_t[:, (16 + 3 * tt) * T:(19 + 3 * tt) * T], Ratom(16 + 3 * tt, 3)) for tt in range(4)]
    qnb = [Buf(R_t[:, (16 + i) * T:(17 + i) * T], Ratom(16 + i)) for i in range(2)]
    PTb = [Buf(R_t[:, (18 + i) * T:(19 + i) * T], Ratom(18 + i)) for i in range(4)]
    PTb6 = PTb + [Buf(R_t[:, (42 + i) * T:(43 + i) * T], Ratom(42 + i)) for i in range(2)]

    def r32(a0):
        return Buf(R_t[:, a0 * T:(a0 + 2) * T].bitcast(F32), Ratom(a0, 2))

    cosb, sinb = r32(22), r32(24)
    tmpA, tmpB, tmpC, tmpD = r32(28), r32(30), r32(32), r32(34)
    kst = Buf(R_t[:, 36 * T:40 * T], Ratom(36, 4))
    vst = Buf(R_t[:, 40 * T:44 * T], Ratom(40, 4))
    xst = Buf(R_t[:, 0:8 * T].bitcast(F32), Ratom(0, 8))
    rstdb = Buf(rstd_t[:, :], [("rstdb", 0)])
    sqbufs = [Buf(sq_t[:, i * T:(i + 1) * T], [("sq", i)]) for i in range(2)]
    silub = [Buf(silu_t[:, i * T:(i + 1) * T], [("silu", i)]) for i in range(2)]
    recipb = Buf(recip_t[:, :], [("recip", 0)])
    lnt = Buf(lnt_t[:, :], [("lnt", 0)])
    KmT = [Buf(memkv_t[:, s * 2048:s * 2048 + 1024], [("KmT", s)]) for s in range(2)]
    Vm = [Buf(memkv_t[:, s * 2048 + 1024:(s + 1) * 2048], [("Vm", s)]) for s in range(2)]
    wspf = Buf(wspf_t[:, :], [("wspf", 0)])
    bspb = Buf(bsp_t[:, :], [("bsp", 0)])
    colsb = Buf(cols_t[:, :], [("cols", 0)])
    colsS = Buf(colsS_t[:, :], [("colsS", 0)])
    onesb = Buf(ones_t[:, :], [("ones", 0)])
    identb = Buf(ident_t[:, :], [("ident", 0)])
    rotb = Buf(rot_t[:, :], [("rot", 0)])
    smallb = Buf(small_t[:, :], [("small", 0)])
    PSb = [Buf(PS_t[:, b * 512:(b + 1) * 512], [("ps", b)]) for b in range(8)]

    st = {"ps": 0, "w": 0, "silu": 0, "qn": 0, "pt": 0, "kv": 0, "sq": 0, "pt6": 0, "acc": 0}

    reserved = set()

    def next_ps():
        while True:
            b = st["ps"]
            st["ps"] = (b + 1) % 8
            if b not in reserved:
                return PSb[b]

    def reserve_ps():
        p = next_ps()
        reserved.add(PSb.index(p))
        return p

    def release_ps(p):
        reserved.discard(PSb.index(p))

    def atoms(*bufs):
        out = []
        for b in bufs:
            out.extend(b.atoms)
        return out

    def mm(ps_ap, lhsT, rhs, start, stop, reads, writes):
        S.op("pe", lambda e: e.matmul(ps_ap, lhsT, rhs, start=start, stop=stop), reads=reads, writes=writes,
             lazy=(not stop) and LAZY_PE)

    def dma(q, out_ap, in_ap, reads, writes, sem):
        if sem not in S.sem_names:
            S.new_sem(sem)
        return S.op(q, lambda e: e.dma_start(out=out_ap, in_=in_ap), reads=reads, writes=writes, sem=sem, inc=16)

    def _small(ap, **kw):
        n = 1
        for d in ap.shape[1:]:
            n *= d
        return n < 64 or ("accum_out" in kw)

    def act(out_ap, in_ap, func, reads, writes, **kw):
        S.op("act", lambda e: e.activation(out_ap, in_ap, func, **kw), reads=reads, writes=writes,
             small=_small(out_ap, **kw))

    def dve(fn, reads, writes):
        S.op("dve", fn, reads=reads, writes=writes)

    def dve_tt(out, in0, in1, op, reads, writes):
        S.op("dve", lambda e: e.tensor_tensor(out, in0, in1, op), reads=reads, writes=writes, small=_small(out))

    def dve_ts(out, in0, s1, s2, op0, op1, reads, writes):
        if op1 is None:
            S.op("dve", lambda e: e.tensor_scalar(out, in0, s1, None, op0), reads=reads, writes=writes, small=_small(out))
        else:
            S.op("dve", lambda e: e.tensor_scalar(out, in0, s1, s2, op0, op1), reads=reads, writes=writes, small=_small(out))

    def dve_stt(out, in0, sc, in1, op0, op1, reads, writes):
        S.op("dve", lambda e: e.scalar_tensor_tensor(out, in0, sc, in1, op0, op1), reads=reads, writes=writes, small=_small(out))

    def pool_tt(out, in0, in1, op, reads, writes):
        S.op("pool", lambda e: e.tensor_tensor(out, in0, in1, op), reads=reads, writes=writes)

    def pool_copy(out, in_, reads, writes):
        S.op("pool", lambda e: e.tensor_copy(out, in_), reads=reads, writes=writes)

    def dve_copy(out, in_, reads, writes):
        S.op("dve", lambda e: e.tensor_copy(out, in_), reads=reads, writes=writes, small=_small(out))

    def dve_recip(out, in_, reads, writes):
        S.op("dve", lambda e: e.reciprocal(out, in_), reads=reads, writes=writes, small=_small(out))

    def dve_rsum(out, in_, reads, writes):
        S.op("dve", lambda e: e.reduce_sum(out, in_, AX.X), reads=reads, writes=writes, small=True)

    def pe_tr(out, in_, reads, writes):
        S.op("pe", lambda e: e.transpose(out, in_, identb.ap), reads=reads + identb.atoms, writes=writes)

    def wload(w2d, r0, nk, c0, ncols):
        s = st["w"]
        st["w"] = (s + 1) % NW
        wgen[s] = wgen.get(s, 0) + 1
        assert ncols == 256 and c0 % 256 == 0
        flat = Wb[s].ap[:, 0:nk * ncols]
        src = w2d[c0 // 256][:, r0 * 256:(r0 + nk) * 256]
        dma("pool", flat, src, reads=[], writes=Wb[s].atoms, sem=f"w{s}")
        return flat.rearrange("p (k c) -> p k c", k=nk), Wb[s]

    wgen = {}

    gcol = lambda c: colsb.ap[:, c:c + 1]
    gcolS = lambda c: colsS.ap[:, c:c + 1]

    dma("sp", colsb.ap, cols_in, [], colsb.atoms, "c0")
    dma("sp", identb.ap, ident_in, [], identb.atoms, "c1")
    dma("pool", rotb.ap, rot_in, [], rotb.atoms, "c2")
    S.op("dve", lambda e: e.memset(onesb.ap, 1.0), writes=onesb.atoms)
    ones32_t = salloc("ones32", [128, 128], F32)
    ones32 = Buf(ones32_t[:, :], [("ones32", 0)])
    S.op("dve", lambda e: e.memset(ones32.ap, 1.0), writes=ones32.atoms)
    epst = salloc("epst", [128, 4], F32)
    epsb = {}
    for i, n in enumerate((2048, 128, 1536)):
        S.op("dve", (lambda i=i, n=n: (lambda e: e.memset(epst[:, i:i + 1], float(n * EPS))))(), writes=[("eps", 0)], small=True)
        epsb[n] = epst[:, i:i + 1]
    for (c0, c1, n) in ((C_GMIX, C_GMQ, 2048.0), (C_GMQ, C_GV, 128.0), (C_GV, NCOLS, 1536.0)):
        dve_ts(colsS.ap[:, c0:c1], colsb.ap[:, c0:c1], float(np.sqrt(n)), None, ALU.mult, None,
               colsb.atoms, colsS.atoms)

    def rstd_from(out_ap, ss_ap, n, reads, writes):
        w = out_ap.shape[-1]
        act(lnt.ap[:, 0:w], ss_ap, AF.Ln, reads=list(reads) + [("eps", 0)], writes=lnt.atoms, bias=epsb[n], scale=1.0)
        act(out_ap, lnt.ap[:, 0:w], AF.Exp, reads=lnt.atoms, writes=writes, scale=-0.5)

    def next_sq():
        b = sqbufs[st["sq"]]
        st["sq"] ^= 1
        return b

    def mem_kv(l, slot):
        hm = hT
        for mt in range(2):
            dma("sp", xst.ap, mem_in[mt * 128:(mt + 1) * 128, :], [], xst.atoms, "xst")
            for q in range(4):
                act(tmpA.ap, xst.ap[:, q * 512:(q + 1) * 512], AF.Square,
                    reads=xst.atoms, writes=tmpA.atoms + smallb.atoms, accum_out=smallb.ap[:, q:q + 1])
            dve_rsum(smallb.ap[:, 4:5], smallb.ap[:, 0:4], smallb.atoms, smallb.atoms)
            rstd_from(smallb.ap[:, 5:6], smallb.ap[:, 4:5], 2048, smallb.atoms, smallb.atoms)
            dve_ts(xst.ap, xst.ap, smallb.ap[:, 5:6], None, ALU.mult, None, xst.atoms + smallb.atoms, xst.atoms)
            for k in range(KC):
                ps = next_ps()
                pe_tr(ps.ap[:, 0:128], xst.ap[:, k * 128:(k + 1) * 128], xst.atoms, ps.atoms)
                act(hm[k].ap[:, mt * 128:(mt + 1) * 128], ps.ap[:, 0:128], AF.Copy,
                    reads=ps.atoms + colsS.atoms, writes=hm[k].atoms, scale=gcolS(C_GMEM + l * 16 + k))
        wl = w_mem_kv[l]
        for half in range(2):
            wv, wb = wload(wl, 0, KC, half * 256, 256)
            for hh in range(2):
                h = half * 2 + hh
                ps = next_ps()
                for k in range(KC):
                    mm(ps.ap[:, 0:256], wv[:, k, hh * 128:(hh + 1) * 128], hm[k].ap[:, 0:256], k == 0, k == KC - 1,
                       reads=wb.atoms + hm[k].atoms, writes=ps.atoms)
                head_norm(ps, 256, C_GMK + l, KmT[slot].ap[:, h * 256:(h + 1) * 256], KmT[slot].atoms)
        pss = [next_ps(), next_ps()]
        for half in range(2):
            wv, wb = wload(wl, 0, KC, 512 + half * 256, 256)
            for mt in range(2):
                for k in range(KC):
                    mm(pss[mt].ap[:, half * 256:(half + 1) * 256], hm[k].ap[:, mt * 128:(mt + 1) * 128], wv[:, k, :],
                       k == 0, k == KC - 1, reads=wb.atoms + hm[k].atoms, writes=pss[mt].atoms)
        for mt in range(2):
            act(Vm[slot].ap[:, mt * 512:(mt + 1) * 512], pss[mt].ap, AF.Copy, reads=pss[mt].atoms, writes=Vm[slot].atoms)

    def norm_block(gbase):
        ps = next_ps()
        for k in range(KC):
            sq = next_sq()
            act(sq.ap, xT[k].ap, AF.Square, reads=xT[k].atoms, writes=sq.atoms)
            mm(ps.ap, onesb.ap, sq.ap, k == 0, k == KC - 1, reads=onesb.atoms + sq.atoms, writes=ps.atoms)
        rstd_from(rstdb.ap, ps.ap, 2048, ps.atoms, rstdb.atoms)
        for k in range(KC):
            dve_stt(hT[k].ap, xT[k].ap, gcolS(gbase + k), rstdb.ap, ALU.mult, ALU.mult,
                    xT[k].atoms + rstdb.atoms + colsS.atoms, hT[k].atoms)

    def head_norm(ps, n, gidx, out_ap, out_atoms):
        sq = next_sq()
        act(sq.ap[:, 0:n], ps.ap[:, 0:n], AF.Square, reads=ps.atoms, writes=sq.atoms)
        ps2 = next_ps()
        mm(ps2.ap[:, 0:n], onesb.ap, sq.ap[:, 0:n], True, True, reads=onesb.atoms + sq.atoms, writes=ps2.atoms)
        rstd_from(rstdb.ap[:, 0:n], ps2.ap[:, 0:n], 128, ps2.atoms, rstdb.atoms)
        dve_stt(out_ap, ps.ap[:, 0:n], gcolS(gidx), rstdb.ap[:, 0:n], ALU.mult, ALU.mult,
                ps.atoms + rstdb.atoms + colsS.atoms, out_atoms)

    def rope(qb, out_ap, out_atoms):
        ps = next_ps()
        mm(ps.ap, rotb.ap, qb.ap, True, True, reads=rotb.atoms + qb.atoms, writes=ps.atoms)
        dve_tt(tmpA.ap, qb.ap, cosb.ap, ALU.mult, qb.atoms + cosb.atoms, tmpA.atoms)
        dve_tt(tmpB.ap, ps.ap, sinb.ap, ALU.mult, ps.atoms + sinb.atoms, tmpB.atoms)
        dve_tt(out_ap, tmpA.ap, tmpB.ap, ALU.add, tmpA.atoms + tmpB.atoms, out_atoms)

    def proj_norm4(wl, c0, gidx, rst, sqs, outs, rope_to=None):
        psq = []
        for half in range(2):
            wv, wb = wload(wl, 0, KC, c0 + half * 256, 256)
            for hh in range(2):
                ps = next_ps()
                psq.append(ps)
                for k in range(KC):
                    mm(ps.ap, wv[:, k, hh * 128:(hh + 1) * 128], hT[k].ap, k == 0, k == KC - 1,
                       reads=wb.atoms + hT[k].atoms, writes=ps.atoms)
        for h in range(4):
            act(sqs[h].ap, psq[h].ap, AF.Square, reads=psq[h].atoms, writes=sqs[h].atoms)
        ps2 = []
        for h in range(4):
            p = next_ps()
            ps2.append(p)
            mm(p.ap, onesb.ap, sqs[h].ap, True, True, reads=onesb.atoms + sqs[h].atoms, writes=p.atoms)
        for h in range(4):
            rstd_from(rst[h].ap, ps2[h].ap, 128, ps2[h].atoms, rst[h].atoms)
        for h in range(4):
            dve_stt(outs[h].ap, psq[h].ap, gcolS(gidx), rst[h].ap, ALU.mult, ALU.mult,
                    psq[h].atoms + rst[h].atoms + colsS.atoms, outs[h].atoms)
        if rope_to is not None:
            for h in range(4):
                mm(ps2[h].ap, rotb.ap, outs[h].ap, True, True, reads=rotb.atoms + outs[h].atoms, writes=ps2[h].atoms)
            for h in range(4):
                o_ap, o_atoms = rope_to[h]
                dve_tt(tmpC.ap, outs[h].ap, cosb.ap, ALU.mult, outs[h].atoms + cosb.atoms, tmpC.atoms)
                dve_tt(tmpD.ap, ps2[h].ap, sinb.ap, ALU.mult, ps2[h].atoms + sinb.atoms, tmpD.atoms)
                dve_tt(o_ap, tmpC.ap, tmpD.ap, ALU.add, tmpC.atoms + tmpD.atoms, o_atoms)

    def mem_core(slot, qbufs):
        def mem_S(h):
            pts = []
            for c in range(2):
                pss = next_ps()
                mm(pss.ap, KmT[slot].ap[:, h * 256 + c * 128:h * 256 + (c + 1) * 128], qbufs[h].ap, True, True,
                   reads=KmT[slot].atoms + qbufs[h].atoms, writes=pss.atoms)
                pt = next_pt()
                act(pt.ap, pss.ap, AF.Exp, reads=pss.atoms, writes=pt.atoms, scale=float(SCALE))
                pts.append(pt)
            return pts

        nxt = mem_S(0)
        for h in range(4):
            pts = nxt
            pso2, psm2 = next_ps(), next_ps()
            if h + 1 < 4:
                reserved.update({PSb.index(pso2), PSb.index(psm2)})
                nxt = mem_S(h + 1)
                reserved.difference_update({PSb.index(pso2), PSb.index(psm2)})
            for c in range(2):
                mm(pso2.ap, Vm[slot].ap[:, c * 512 + h * 128:c * 512 + (h + 1) * 128], pts[c].ap, c == 0, c == 1,
                   reads=Vm[slot].atoms + pts[c].atoms, writes=pso2.atoms)
            for c in range(2):
                mm(psm2.ap, onesb.ap, pts[c].ap, c == 0, c == 1, reads=onesb.atoms + pts[c].atoms, writes=psm2.atoms)
            dve_recip(recipb.ap, psm2.ap, psm2.atoms, recipb.atoms)
            dve_tt(catT[12 + h].ap, pso2.ap, recipb.ap, ALU.mult, pso2.atoms + recipb.atoms, catT[12 + h].atoms)

    def next_qn():
        b = qnb[st["qn"]]
        st["qn"] ^= 1
        return b

    def next_pt():
        b = PTb[st["pt"] % 4]
        st["pt"] = (st["pt"] + 1) % 4
        return b

    def next_pt6():
        b = PTb6[st["pt6"]]
        st["pt6"] = (st["pt6"] + 1) % 6
        return b

    def mem_attention(l, slot, wl, qc0):
        for half in range(2):
            wv, wb = wload(wl, 0, KC, qc0 + half * 256, 256)
            for hh in range(2):
                h = half * 2 + hh
                ps = next_ps()
                for k in range(KC):
                    mm(ps.ap, wv[:, k, hh * 128:(hh + 1) * 128], hT[k].ap, k == 0, k == KC - 1,
                       reads=wb.atoms + hT[k].atoms, writes=ps.atoms)
                qb = next_qn()
                head_norm(ps, T, C_GMQ + l, qb.ap, qb.atoms)
                pts = []
                for c in range(2):
                    pss = next_ps()
                    mm(pss.ap, KmT[slot].ap[:, h * 256 + c * 128:h * 256 + (c + 1) * 128], qb.ap, True, True,
                       reads=KmT[slot].atoms + qb.atoms, writes=pss.atoms)
                    pt = next_pt()
                    act(pt.ap, pss.ap, AF.Exp, reads=pss.atoms, writes=pt.atoms, scale=float(SCALE))
                    pts.append(pt)
                pso, psm = next_ps(), next_ps()
                for c in range(2):
                    mm(pso.ap, Vm[slot].ap[:, c * 512 + h * 128:c * 512 + (h + 1) * 128], pts[c].ap, c == 0, c == 1,
                       reads=Vm[slot].atoms + pts[c].atoms, writes=pso.atoms)
                for c in range(2):
                    mm(psm.ap, onesb.ap, pts[c].ap, c == 0, c == 1, reads=onesb.atoms + pts[c].atoms, writes=psm.atoms)
                dve_recip(recipb.ap, psm.ap, psm.atoms, recipb.atoms)
                dve_tt(catT[12 + h].ap, pso.ap, recipb.ap, ALU.mult, pso.atoms + recipb.atoms, catT[12 + h].atoms)

    def out_proj(l):
        wl = w_out[l]
        for mp in range(8):
            wv, wb = wload(wl, 0, KC, mp * 256, 256)
            for mi in range(2):
                m = mp * 2 + mi
                ps = next_ps()
                for c in range(KC):
                    mm(ps.ap, wv[:, c, mi * 128:(mi + 1) * 128], catT[c].ap, c == 0, c == KC - 1,
                       reads=wb.atoms + catT[c].atoms, writes=ps.atoms)
                dve_tt(xT[m].ap, xT[m].ap, ps.ap, ALU.add, xT[m].atoms + ps.atoms, xT[m].atoms)

    def ffn(l):
        norm_block(C_GFFN + l * 16)
        wl = w_gu[l]
        for jp in range(FC // 2):
            wg, wgb = wload(wl, 0, KC, jp * 256, 256)
            wu, wub = wload(wl, 0, KC, DFF + jp * 256, 256)
            for ji in range(2):
                j = jp * 2 + ji
                psg, psu = next_ps(), next_ps()
                for k in range(KC):
                    mm(psg.ap, wg[:, k, ji * 128:(ji + 1) * 128], hT[k].ap, k == 0, k == KC - 1,
                       reads=wgb.atoms + hT[k].atoms, writes=psg.atoms)
                for k in range(KC):
                    mm(psu.ap, wu[:, k, ji * 128:(ji + 1) * 128], hT[k].ap, k == 0, k == KC - 1,
                       reads=wub.atoms + hT[k].atoms, writes=psu.atoms)
                sb_ = silub[st["silu"]]
                st["silu"] ^= 1
                act(sb_.ap, psg.ap, AF.Silu, reads=psg.atoms, writes=sb_.atoms)
                dve_tt(actT[j].ap, sb_.ap, psu.ap, ALU.mult, sb_.atoms + psu.atoms, actT[j].atoms)
        wl = w_dn[l]
        fsplit = [(0, 16), (16, 16), (32, 12)]
        for mp in range(8):
            pss = [next_ps(), next_ps()]
            for (f0, nf) in fsplit:
                wv, wb = wload(wl, f0, nf, mp * 256, 256)
                for mi in range(2):
                    for f in range(nf):
                        fg = f0 + f
                        mm(pss[mi].ap, wv[:, f, mi * 128:(mi + 1) * 128], actT[fg].ap, fg == 0, fg == FC - 1,
                           reads=wb.atoms + actT[fg].atoms, writes=pss[mi].atoms)
            for mi in range(2):
                m = mp * 2 + mi
                dve_tt(xT[m].ap, xT[m].ap, pss[mi].ap, ALU.add, xT[m].atoms + pss[mi].atoms, xT[m].atoms)

    wsc_all = [Buf(R_t[:, (32 + 3 * tt) * T:(35 + 3 * tt) * T], Ratom(32 + 3 * tt, 3)) for tt in range(4)]

    def layer_a_mixer(l, slot):
        ia = l // 2
        wl = w_in_a[ia]
        norm_block(C_GMIX + l * 16)
        dma("sp", wspf.ap, w_spT[ia], [], wspf.atoms, "wspf")
        dma("sp", bspb.ap, bsp_in[ia], [], bspb.atoms, "bsp")
        for wt in range(6):
            wv, wb = wload(wl, 0, KC, TW + wt * 256, 256)
            for tt in range(4):
                ps = next_ps()
                for k in range(KC):
                    mm(ps.ap[:, 0:256], hT[k].ap[:, tt * 128:(tt + 1) * 128], wv[:, k, :], k == 0, k == KC - 1,
                       reads=wb.atoms + hT[k].atoms, writes=ps.atoms)
                act(tmpA.ap[:, 0:256], ps.ap[:, 0:256], AF.Gelu, reads=ps.atoms, writes=tmpA.atoms)
                act(tmpB.ap[:, 0:256], tmpA.ap[:, 0:256], AF.Square, reads=tmpA.atoms, writes=tmpB.atoms + smallb.atoms,
                    accum_out=smallb.ap[:, 8 + tt * 6 + wt:9 + tt * 6 + wt])
                dve_copy(vn[tt].ap[:, wt * 256:(wt + 1) * 256], tmpA.ap[:, 0:256], tmpA.atoms, vn[tt].atoms)
        dve_rsum(smallb.ap[:, 32:36], smallb.ap[:, 8:32].rearrange("p (a b) -> p a b", a=4), smallb.atoms, smallb.atoms)
        rstd_from(smallb.ap[:, 36:40], smallb.ap[:, 32:36], 1536, smallb.atoms, smallb.atoms)
        for tt in range(4):
            dve_ts(wsc_all[tt].ap, wspf.ap, smallb.ap[:, 36 + tt:37 + tt], None, ALU.mult, None,
                   wspf.atoms + smallb.atoms, wsc_all[tt].atoms)
        for up in range(6):
            wv, wb = wload(wl, 0, KC, up * 256, 256)
            for gi in range(2):
                g = up * 2 + gi
                psu = next_ps()
                for k in range(KC):
                    mm(psu.ap, wv[:, k, gi * 128:(gi + 1) * 128], hT[k].ap, k == 0, k == KC - 1,
                       reads=wb.atoms + hT[k].atoms, writes=psu.atoms)
                act(tmpA.ap, psu.ap, AF.Gelu, reads=psu.atoms, writes=tmpA.atoms)
                pss = next_ps()
                for tt in range(4):
                    mm(pss.ap[:, tt * 128:(tt + 1) * 128], vn[tt].ap[:, g * 128:(g + 1) * 128],
                       wsc_all[tt].ap[:, g * 128:(g + 1) * 128], True, True,
                       reads=vn[tt].atoms + wsc_all[tt].atoms, writes=pss.atoms)
                dve_stt(tmpB.ap.rearrange("p (a b) -> p a b", a=4), pss.ap.rearrange("p (a b) -> p a b", a=4),
                        gcolS(C_GV + ia * 12 + g),
                        bspb.ap[:, g * 128:(g + 1) * 128].unsqueeze(1).broadcast_to([128, 4, 128]),
                        ALU.mult, ALU.add, pss.atoms + colsS.atoms + bspb.atoms, tmpB.atoms)
                dve_tt(catT[g].ap, tmpA.ap, tmpB.ap, ALU.mult, tmpA.atoms + tmpB.atoms, catT[g].atoms)
        qbufs = [Buf(R_t[:, (22 + h) * T:(23 + h) * T], Ratom(22 + h)) for h in range(4)]
        proj_norm4(wl, 2 * TW, C_GMQ + l, [r32(32), r32(34), r32(36), r32(38)], PTb, qbufs)
        mem_core(slot, qbufs)

    def load_rope_tables(blk):
        dma("sp", cosb.ap, cos_in[:, blk * T:(blk + 1) * T], [], cosb.atoms, "cos")
        dma("sp", sinb.ap, sin_in[:, blk * T:(blk + 1) * T], [], sinb.atoms, "sin")

    def layer_b_kv(l, blk, e):
        ib = l // 2
        wl = w_in_b[ib]
        norm_block(C_GMIX + l * 16)
        load_rope_tables(blk)
        kq = [Buf(R_t[:, (8 + h) * T:(9 + h) * T], Ratom(8 + h)) for h in range(4)]
        proj_norm4(wl, TW, C_GK + ib, [r32(0), r32(2), r32(4), r32(6)], PTb, kq,
                   rope_to=[(kst.ap[:, h * T:(h + 1) * T], kst.atoms) for h in range(4)])
        dma("sp", kT_loc[e].rearrange("(h d) t -> d h t", d=128)[:, :, blk * T:(blk + 1) * T],
            kst.ap.rearrange("p (h t) -> p h t", h=4), kst.atoms, [("kTloc", e, blk)], "kst")
        pss = [next_ps() for _ in range(4)]
        for half in range(2):
            wv, wb = wload(wl, 0, KC, TW + 512 + half * 256, 256)
            for tt in range(4):
                for k in range(KC):
                    mm(pss[tt].ap[:, half * 256:(half + 1) * 256], hT[k].ap[:, tt * 128:(tt + 1) * 128], wv[:, k, :],
                       k == 0, k == KC - 1, reads=wb.atoms + hT[k].atoms, writes=pss[tt].atoms)
        for tt in range(4):
            act(vst.ap[:, tt * T:(tt + 1) * T], pss[tt].ap, AF.Copy, reads=pss[tt].atoms, writes=vst.atoms)
        dma("sp", v_loc[e][blk * T:(blk + 1) * T, :].rearrange("(tt p) c -> p tt c", p=128),
            vst.ap.rearrange("p (tt c) -> p tt c", tt=4), vst.atoms, [("vloc", e, blk)], "vst")

    qrb = [Buf(tmpC.ap.bitcast(BF16)[:, 0:T], tmpC.atoms), Buf(tmpD.ap.bitcast(BF16)[:, 0:T], tmpD.atoms)]

    qmnb = [Buf(R_t[:, (36 + i) * T:(37 + i) * T], Ratom(36 + i)) for i in range(4)]
    accb = [r32(26), r32(40)]

    def layer_b_attn(l, blk, e, slot):
        ib = l // 2
        wl = w_in_b[ib]
        norm_block(C_GMIX + l * 16)
        load_rope_tables(blk)
        prepA, prepB = reserve_ps(), reserve_ps()
        wq = {}

        def qtile(c0):
            cb = (c0 // 256) * 256
            if cb in wq:
                wv, wb, g = wq[cb]
                if wgen[Wb.index(wb)] != g:
                    del wq[cb]
            if cb not in wq:
                wv, wb = wload(wl, 0, KC, cb, 256)
                wq[cb] = (wv, wb, wgen[Wb.index(wb)])
            wv, wb, _ = wq[cb]
            return wv, wb, c0 - cb

        def prep_gen(c0, gidx, out_buf, do_rope):
            wv, wb, off = qtile(c0)
            for k in range(KC):
                mm(prepA.ap, wv[:, k, off:off + 128], hT[k].ap, k == 0, k == KC - 1,
                   reads=wb.atoms + hT[k].atoms, writes=prepA.atoms)
                if k % 2 == 1:
                    yield
            sq = next_sq()
            act(sq.ap, prepA.ap, AF.Square, reads=prepA.atoms, writes=sq.atoms)
            mm(prepB.ap, onesb.ap, sq.ap, True, True, reads=onesb.atoms + sq.atoms, writes=prepB.atoms)
            yield
            rstd_from(rstdb.ap, prepB.ap, 128, prepB.atoms, rstdb.atoms)
            if do_rope:
                qb = next_qn()
                dve_stt(qb.ap, prepA.ap, gcolS(gidx), rstdb.ap, ALU.mult, ALU.mult,
                        prepA.atoms + rstdb.atoms + colsS.atoms, qb.atoms)
                mm(prepB.ap, rotb.ap, qb.ap, True, True, reads=rotb.atoms + qb.atoms, writes=prepB.atoms)
                yield
                dve_tt(tmpA.ap, qb.ap, cosb.ap, ALU.mult, qb.atoms + cosb.atoms, tmpA.atoms)
                dve_tt(tmpB.ap, prepB.ap, sinb.ap, ALU.mult, prepB.atoms + sinb.atoms, tmpB.atoms)
                dve_tt(out_buf.ap, tmpA.ap, tmpB.ap, ALU.add, tmpA.atoms + tmpB.atoms, out_buf.atoms)
            else:
                dve_stt(out_buf.ap, prepA.ap, gcolS(gidx), rstdb.ap, ALU.mult, ALU.mult,
                        prepA.atoms + rstdb.atoms + colsS.atoms, out_buf.atoms)
            yield

        def run_all(g):
            for _ in g:
                pass

        def q_gen(qh):
            return prep_gen(qh * 128, C_GQ + ib, qrb[qh % 2], True)

        def m_gen(h):
            return prep_gen(TW + 1024 + h * 128, C_GMQ + l, qmnb[h], False)

        qtile(0)
        run_all(q_gen(0))
        psos, psm = [reserve_ps(), reserve_ps()], reserve_ps()
        LOOK = 2
        for kvh in range(4):
            s = st["kv"]
            st["kv"] ^= 1
            dma("sp", KTb[s].ap.rearrange("p (r t) -> p r t", r=2),
                kT_all[e].rearrange("(r h d) t -> h d r t", r=2, h=4)[kvh],
                [("kTall", e)], KTb[s].atoms, f"kt{s}")
            for jq in range(4):
                dma("sp", Vb[s].ap.rearrange("p (j c) -> p j c", j=32)[:, jq * 8:(jq + 1) * 8, :],
                    v_all[e].rearrange("(j p) c -> p j c", p=128)[:, jq * 8:(jq + 1) * 8, kvh * 128:(kvh + 1) * 128],
                    [("vall", e)], Vb[s].atoms, f"vv{s}")
            for qi in range(3):
                qh = kvh * 3 + qi
                qr = qrb[qh % 2]
                pso = psos[qh % 2]
                if qh + 2 < 12:
                    qtile((qh + 2) * 128)
                if 3 <= qh < 7:
                    qtile(TW + 1024 + (qh - 3) * 128)
                gens = []
                sched = {}
                if qh + 1 < 12:
                    g = q_gen(qh + 1)
                    gens.append(g)
                    for jj in list(range(1, 9)) + [10, 14, 18]:
                        sched[jj] = g
                if 4 <= qh < 8:
                    g = m_gen(qh - 4)
                    gens.append(g)
                    for jj in list(range(19, 27)) + [28, 30]:
                        sched[jj] = g

                acc = accb[st["acc"]]
                accp = silub[st["acc"]]
                st["acc"] ^= 1

                def emit_S(j):
                    pss = next_ps()
                    mm(pss.ap, KTb[s].ap[:, j * 128:(j + 1) * 128], qr.ap, True, True,
                       reads=KTb[s].atoms + qr.atoms, writes=pss.atoms)
                    pt = next_pt6()
                    act(pt.ap, pss.ap, AF.Exp, reads=pss.atoms, writes=pt.atoms, scale=float(SCALE))
                    a_ = acc if j % 2 == 0 else accp
                    if j < 2:
                        dve_copy(a_.ap, pt.ap, pt.atoms, a_.atoms)
                    else:
                        dve_tt(a_.ap, a_.ap, pt.ap, ALU.add, a_.atoms + pt.atoms, a_.atoms)
                    return pt

                pts = {}
                for j in range(LOOK):
                    pts[j] = emit_S(j)
                for j in range(32):
                    if j + LOOK < 32:
                        pts[j + LOOK] = emit_S(j + LOOK)
                    pt = pts.pop(j)
                    mm(pso.ap, Vb[s].ap[:, j * 128:(j + 1) * 128], pt.ap, j == 0, j == 31,
                       reads=Vb[s].atoms + pt.atoms, writes=pso.atoms)
                    if j in sched:
                        next(sched[j], None)
                for g in gens:
                    run_all(g)
                dve_tt(acc.ap, acc.ap, accp.ap, ALU.add, acc.atoms + accp.atoms, acc.atoms)
                mm(psm.ap, ones32.ap, acc.ap, True, True, reads=ones32.atoms + acc.atoms, writes=psm.atoms)
                act(lnt.ap, psm.ap, AF.Ln, reads=psm.atoms, writes=lnt.atoms)
                act(recipb.ap, lnt.ap, AF.Exp, reads=lnt.atoms, writes=recipb.atoms, scale=-1.0)
                dve_tt(catT[qh].ap, pso.ap, recipb.ap, ALU.mult, pso.atoms + recipb.atoms, catT[qh].atoms)
        release_ps(prepA)
        release_ps(prepB)
        release_ps(psos[0])
        release_ps(psos[1])
        release_ps(psm)
        mem_core(slot, qmnb)

    def load_x_tokenmajor(blk):
        dma("sp", xT_t[:, :], x_in[blk], [], atoms(*xT), "xpl")

    def store_x_tokenmajor(blk):
        dma("sp", y_out[blk], xT_t[:, :], atoms(*xT), [("yout", blk, 0)], "yst")

    def load_x_park(blk):
        dma("sp", xT_t[:, :], xpark_r[blk], [("xpark", blk)], atoms(*xT), "xpl")

    def store_x_park(blk):
        dma("sp", xpark_w[blk], xT_t[:, :], atoms(*xT), [("xpark", blk)], "xps")

    def exchange(e):
        groups = [[0, 1], [2, 3], [4, 5], [6, 7]]
        rk = [("kTloc", e, b) for b in range(NBLK)]
        rv = [("vloc", e, b) for b in range(NBLK)]
        S.new_sem(f"cck{e}")
        S.new_sem(f"ccv{e}")
        S.op("pool", lambda en: en.collective_compute("AllGather", ALU.bypass, replica_groups=groups,
                                                      ins=[kT_loc[e]], outs=[kT_all[e]]),
             reads=rk, writes=[("kTall", e)], sem=f"cck{e}", inc=CC_INC)
        S.op("pool", lambda en: en.collective_compute("AllGather", ALU.bypass, replica_groups=groups,
                                                      ins=[v_loc[e]], outs=[v_all[e]]),
             reads=rv, writes=[("vall", e)], sem=f"ccv{e}", inc=CC_INC)

    def a_layer(l, slot):
        layer_a_mixer(l, slot)
        out_proj(l)
        ffn(l)

    for ph in phases:
        mem_layers = {0: [0], 1: [1, 2], 2: [3]}[ph]
        slots = {}
        for i, l in enumerate(mem_layers):
            mem_kv(l, i)
            slots[l] = i
        for blk in range(NBLK):
            if ph == 0:
                load_x_tokenmajor(blk)
                a_layer(0, slots[0])
                layer_b_kv(1, blk, 0)
                store_x_park(blk)
            elif ph == 1:
                load_x_park(blk)
                layer_b_attn(1, blk, 0, slots[1])
                out_proj(1)
                ffn(1)
                a_layer(2, slots[2])
                layer_b_kv(3, blk, 1)
                store_x_park(blk)
            else:
                load_x_park(blk)
                layer_b_attn(3, blk, 1, slots[3])
                out_proj(3)
                ffn(3)
                store_x_tokenmajor(blk)
        if fused and ph < 2:
            exchange(ph)

    out_keys = []
    if 2 in phases:
        out_keys += [("yout", b, 0) for b in range(NBLK)]
    if not fused:
        if last_phase < 2:
            out_keys += [("xpark", b) for b in range(NBLK)]
            e = last_phase
            out_keys += [("kTloc", e, b) for b in range(NBLK)] + [("vloc", e, b) for b in range(NBLK)]
    S.wait_all("sp", out_keys)

    sem_cms = {name: nc.semaphore(name) for name in S.sem_names}
    sems = {}
    for name, cm in sem_cms.items():
        sems[name] = cm.__enter__()
        ctxs.append(cm)

    def replay(e, items):
        for it in items:
            if it[0] == "wait":
                e.wait_ge(sems[it[1]], it[2])
            else:
                _, fn, semname, inc = it
                ins = fn(e)
                if inc:
                    ins.then_inc(sems[semname], inc)

    with nc.Block() as block:
        @block.tensor
        def _(e):
            replay(e, S.lists["pe"])

        @block.scalar
        def _(e):
            replay(e, S.lists["act"])

        @block.vector
        def _(e):
            replay(e, S.lists["dve"])

        @block.gpsimd
        def _(e):
            replay(e, S.lists["pool"])

        @block.sync
        def _(e):
            replay(e, S.lists["sp"])

    for cm in reversed(ctxs):
        cm.__exit__(None, None, None)
    return nc, ({k: len(v) for k, v in S.lists.items()} if not Sched.TRACE else S.pe_labels)


CC_INC = 1
LAZY_PE = True


def _slice_weights(com, phases):
    Ls, As, Bs = _weight_sets(phases)
    d = dict(com)
    d["w_in_a"] = np.ascontiguousarray(com["w_in_a"][As]) if As else np.ascontiguousarray(com["w_in_a"][:1])
    d["w_in_b"] = np.ascontiguousarray(com["w_in_b"][Bs]) if Bs else np.ascontiguousarray(com["w_in_b"][:1])
    for k in ("w_mem_kv", "w_out", "w_gate_up", "w_down"):
        d[k] = np.ascontiguousarray(com[k][Ls])
    return d


def _common_inputs(inputs):
    f = lambda a: np.ascontiguousarray(np.asarray(a, dtype=np.float32))
    cols = np.zeros((128, NCOLS), np.float32)

    def colize(v):
        return np.asarray(v, np.float32).reshape(-1, 128).T

    for l in range(4):
        cols[:, C_GMIX + l * 16:C_GMIX + (l + 1) * 16] = colize(inputs["g_mix"][l])
        cols[:, C_GFFN + l * 16:C_GFFN + (l + 1) * 16] = colize(inputs["g_ffn"][l])
        cols[:, C_GMEM + l * 16:C_GMEM + (l + 1) * 16] = colize(inputs["g_mem"][l])
        cols[:, C_GMQ + l] = np.asarray(inputs["g_mq"][l], np.float32)
        cols[:, C_GMK + l] = np.asarray(inputs["g_mk"][l], np.float32)
    for i in range(2):
        cols[:, C_GQ + i] = np.asarray(inputs["g_q_b"][i], np.float32)
        cols[:, C_GK + i] = np.asarray(inputs["g_k_b"][i], np.float32)
        cols[:, C_GV + i * 12:C_GV + (i + 1) * 12] = colize(inputs["g_v_a"][i])
    ident, rot = _host_consts()
    wsp = np.asarray(inputs["w_spatial"], np.float32)
    w_spT = np.ascontiguousarray(wsp.transpose(0, 3, 1, 2).reshape(2, 128, TW))
    bsp = np.ascontiguousarray(np.broadcast_to(np.asarray(inputs["b_spatial"], np.float32).reshape(2, 1, TW), (2, 128, TW)))
    def tile(w):
        w = np.asarray(w, np.float32)
        L, R, C = w.shape
        nk = R // 128
        return np.ascontiguousarray(w.reshape(L, nk, 128, C // 256, 256).transpose(0, 3, 2, 1, 4)).reshape(L, C // 256, 128, nk * 256)

    com = {
        "cols": cols, "ident": ident, "rot": rot, "w_spT": w_spT, "bsp": bsp,
        "w_in_a": tile(inputs["w_in_a"]), "w_in_b": tile(inputs["w_in_b"]), "w_mem_kv": tile(inputs["w_mem_kv"]),
        "w_out": tile(inputs["w_out"]), "w_gate_up": tile(inputs["w_gate_up"]), "w_down": tile(inputs["w_down"]),
    }
    return com


_CACHE = {}


def _get_prog(phases, fused):
    key = (tuple(phases), fused)
    if key not in _CACHE:
        _CACHE[key] = build(list(phases), fused)[0]
    return _CACHE[key]


def _to_blocks(xs):
    return np.ascontiguousarray(xs.reshape(NBLK, T, KC, 128).transpose(0, 3, 2, 1)).reshape(NBLK, 128, KC * T)


def _from_blocks(yb):
    return yb.reshape(NBLK, 128, KC, T).transpose(0, 3, 2, 1).reshape(TOK, D)


def kernel(**inputs):
    x = np.asarray(inputs["x"], np.float32)
    mem = np.asarray(inputs["mem"], np.float32)
    com = _common_inputs(inputs)
    cosT, sinT = _rope_tables()
    per_core = []
    for c in range(8):
        b, half = c // 2, c % 2
        d = dict(com)
        d["mem"] = np.ascontiguousarray(mem[b])
        d["cosT"] = np.ascontiguousarray(cosT[:, half * TOK:(half + 1) * TOK])
        d["sinT"] = np.ascontiguousarray(sinT[:, half * TOK:(half + 1) * TOK])
        per_core.append(d)
    out = np.empty((4, SEQ, D), np.float32)
    if FUSED:
        nc = _get_prog((0, 1, 2), True)
        in_maps = []
        for c in range(8):
            b, half = c // 2, c % 2
            d = dict(per_core[c])
            d["x_in"] = _to_blocks(x[b, half * TOK:(half + 1) * TOK])
            in_maps.append(d)
        res = run_bass_kernel_spmd(nc, in_maps, core_ids=list(range(8)))
        for c in range(8):
            b, half = c // 2, c % 2
            out[b, half * TOK:(half + 1) * TOK] = _from_blocks(np.asarray(res.results[c]["y_out"]))
        return out
    state = [dict() for _ in range(8)]
    for ph in range(3):
        nc = _get_prog((ph,), False)
        wsl = _slice_weights(com, (ph,))
        in_maps = []
        for c in range(8):
            b, half = c // 2, c % 2
            d = dict(per_core[c])
            d.update({k: wsl[k] for k in ("w_in_a", "w_in_b", "w_mem_kv", "w_out", "w_gate_up", "w_down")})
            if ph == 0:
                d["x_in"] = _to_blocks(x[b, half * TOK:(half + 1) * TOK])
            else:
                d["xpark_in"] = state[c]["xpark"]
                e = ph - 1
                p0, p1 = state[2 * b], state[2 * b + 1]
                d[f"kT_all{e}"] = np.concatenate([p0["kT"], p1["kT"]], axis=0)
                d[f"v_all{e}"] = np.concatenate([p0["v"], p1["v"]], axis=0)
            in_maps.append(d)
        res = run_bass_kernel_spmd(nc, in_maps, core_ids=list(range(8)))
        new_state = [dict() for _ in range(8)]
        for c in range(8):
            r = res.results[c]
            if ph < 2:
                new_state[c]["xpark"] = r["xpark_out"]
                new_state[c]["kT"] = r[f"kT_loc{ph}"]
                new_state[c]["v"] = r[f"v_loc{ph}"]
            else:
                b, half = c // 2, c % 2
                out[b, half * TOK:(half + 1) * TOK] = _from_blocks(np.asarray(r["y_out"]))
        state = new_state
    return out
```

```python
import numpy as np
import concourse.bass as bass
import concourse.mybir as mybir
from concourse.bass_utils import run_bass_kernel_spmd

F32 = mybir.dt.float32
BF16 = mybir.dt.bfloat16
AF = mybir.ActivationFunctionType
ALU = mybir.AluOpType
AX = mybir.AxisListType

D = 2048
KC = 16
T = 512
NBLK = 4
TOK = 2048
SEQ = 4096
DFF = 5632
FC = 44
TW = 1536
HD = 128
EPS = 1e-6
N_MEM = 256
SCALE = HD ** -0.5

C_GMIX, C_GFFN, C_GMEM, C_GMQ, C_GMK, C_GQ, C_GK, C_GV, NCOLS = 0, 64, 128, 192, 196, 200, 202, 204, 228

FUSED = True


class Tok:
    __slots__ = ("sem", "v", "entry", "small")

    def __init__(self, sem, v, entry=None, small=False):
        self.sem, self.v, self.entry, self.small = sem, v, entry, small


class Sched:
    ENG = ("pe", "act", "dve", "pool", "sp")
    TRACE = False

    def __init__(self):
        self.lists = {e: [] for e in self.ENG}
        self.cnt = {}
        self.waited = {}
        self.last_w = {}
        self.readers = {}
        self.sem_names = list(self.ENG)
        self.pe_labels = []
        self.pending = {e: [] for e in self.ENG}

    def new_sem(self, name):
        assert name not in self.sem_names
        self.sem_names.append(name)
        return name

    def _resolve(self, tok):
        if tok.v is not None:
            return
        pend = self.pending[tok.sem]
        i = pend.index(tok)
        tok.entry[3] = 1
        v = self.cnt.get(tok.sem, 0) + 1
        self.cnt[tok.sem] = v
        for t in pend[:i + 1]:
            t.v = v
        del pend[:i + 1]

    def _collect(self, reads, writes):
        toks = []
        for k in reads:
            t = self.last_w.get(k)
            if t is not None:
                toks.append(t)
        for k in writes:
            t = self.last_w.get(k)
            if t is not None:
                toks.append(t)
            toks.extend(self.readers.get(k, ()))
        return toks

    def _emit_waits(self, eng, toks, skip_same=True):
        deps = {}
        for t in toks:
            if skip_same and t.sem == eng:
                if eng == "pe":
                    continue
                if t.v is not None and self.cnt.get(eng, 0) - t.v >= 4:
                    continue
            self._resolve(t)
            if deps.get(t.sem, 0) < t.v:
                deps[t.sem] = t.v
        for s_, v in deps.items():
            if self.waited.get((eng, s_), 0) >= v:
                continue
            self.waited[(eng, s_)] = v
            self.lists[eng].append(("wait", s_, v))

    def op(self, eng, fn, reads=(), writes=(), sem=None, inc=None, lazy=False, small=False):
        semname = sem or eng
        if inc is None:
            inc = 1 if sem is None else 16
        self._emit_waits(eng, self._collect(reads, writes))
        if lazy:
            entry = ["op", fn, semname, 0]
            tok = Tok(semname, None, entry)
            self.pending[semname].append(tok)
        else:
            v = self.cnt.get(semname, 0) + inc
            self.cnt[semname] = v
            entry = ["op", fn, semname, inc]
            tok = Tok(semname, v, small=small)
            if semname in self.pending:
                for t in self.pending[semname]:
                    t.v = v
                self.pending[semname] = []
        self.lists[eng].append(entry)
        if eng == "pe" and Sched.TRACE:
            import sys as _sys
            f = _sys._getframe(1)
            names = []
            while f is not None and len(names) < 8:
                names.append(f"{f.f_code.co_name}:{f.f_lineno}")
                f = f.f_back
            self.pe_labels.append(names)
        for k in writes:
            self.last_w[k] = tok
            self.readers[k] = []
        for k in reads:
            self.readers.setdefault(k, []).append(tok)
        return tok

    def wait_all(self, eng, keys):
        toks = []
        for k in keys:
            t = self.last_w.get(k)
            if t is not None:
                toks.append(t)
            toks.extend(self.readers.get(k, ()))
        self._emit_waits(eng, toks, skip_same=False)


class Buf:
    def __init__(self, ap, atoms):
        self.ap = ap
        self.atoms = list(atoms)


def _weight_sets(phases):
    Ls, As, Bs = [], [], []
    for ph in phases:
        l_, a_, b_ = {0: ([0], [0], [0]), 1: ([1, 2], [1], [0, 1]), 2: ([3], [], [1])}[ph]
        Ls += [x for x in l_ if x not in Ls]
        As += [x for x in a_ if x not in As]
        Bs += [x for x in b_ if x not in Bs]
    return sorted(Ls), sorted(As), sorted(Bs)


def _host_consts():
    ident = np.eye(128, dtype=np.float32)
    rot = np.zeros((128, 128), np.float32)
    for a in range(2):
        for p in range(32):
            rot[a * 64 + 32 + p, a * 64 + p] = -1.0
            rot[a * 64 + p, a * 64 + 32 + p] = 1.0
    return ident, rot


def _rope_tables():
    n_rows = SEQ // 64
    pos = np.arange(SEQ)
    rows = (pos // 64).astype(np.float32)
    cols = (pos % 64).astype(np.float32)
    freqs = (np.float32(10000.0) ** (-np.arange(32, dtype=np.float32) / np.float32(32))).astype(np.float32)
    ang_r = rows[:, None] * freqs
    ang_c = cols[:, None] * freqs
    ang = np.concatenate([ang_r, ang_r, ang_c, ang_c], axis=-1).astype(np.float32)
    return np.ascontiguousarray(np.cos(ang).T.astype(np.float32)), np.ascontiguousarray(np.sin(ang).T.astype(np.float32))


def build(phases, fused):
    nc = bass.Bass("TRN2", target_bir_lowering=False)
    S = Sched()

    def din(name, shape, dt=F32):
        return nc.dram_tensor(name, list(shape), dt, kind="ExternalInput").ap()

    def dout(name, shape, dt=F32):
        return nc.dram_tensor(name, list(shape), dt, kind="ExternalOutput").ap()

    def dint(name, shape, dt=F32):
        return nc.dram_tensor(name, list(shape), dt).ap()

    first_phase, last_phase = phases[0], phases[-1]
    x_in = din("x_in", [NBLK, 128, KC * T]) if 0 in phases else None
    y_out = dout("y_out", [NBLK, 128, KC * T]) if 2 in phases else None
    mem_in = din("mem", [N_MEM, D])
    cols_in = din("cols", [128, NCOLS])
    ident_in = din("ident", [128, 128])
    rot_in = din("rot", [128, 128])
    cos_in = din("cosT", [128, TOK])
    sin_in = din("sinT", [128, TOK])
    Ls, As, Bs = _weight_sets(phases)
    Lmap = {l: i for i, l in enumerate(Ls)}
    Amap = {a: i for i, a in enumerate(As)}
    Bmap = {b: i for i, b in enumerate(Bs)}

    class _Idx:
        def __init__(self, ap, m):
            self.ap, self.m = ap, m

        def __getitem__(self, i):
            return self.ap[self.m[i]]

    w_in_a = _Idx(din("w_in_a", [max(len(As), 1), 14, 128, KC * 256]), Amap)
    w_in_b = _Idx(din("w_in_b", [max(len(Bs), 1), 12, 128, KC * 256]), Bmap)
    w_spT = din("w_spT", [2, 128, TW])
    bsp_in = din("bsp", [2, 128, TW])
    w_mem_kv = _Idx(din("w_mem_kv", [len(Ls), 4, 128, KC * 256]), Lmap)
    w_out = _Idx(din("w_out", [len(Ls), 8, 128, KC * 256]), Lmap)
    w_gu = _Idx(din("w_gate_up", [len(Ls), 44, 128, KC * 256]), Lmap)
    w_dn = _Idx(din("w_down", [len(Ls), 8, 128, FC * 256]), Lmap)

    if fused:
        xpark = dint("xpark", [NBLK, 128, KC * T])
        xpark_r = xpark_w = xpark
        kT_loc = [dint(f"kT_loc{e}", [512, TOK], BF16) for e in range(2)]
        v_loc = [dint(f"v_loc{e}", [TOK, 512], BF16) for e in range(2)]
        kT_all = [dint(f"kT_all{e}", [1024, TOK], BF16) for e in range(2)]
        v_all = [dint(f"v_all{e}", [SEQ, 512], BF16) for e in range(2)]
    else:
        xpark_r = din("xpark_in", [NBLK, 128, KC * T]) if first_phase > 0 else None
        xpark_w = dout("xpark_out", [NBLK, 128, KC * T]) if last_phase < 2 else None
        kT_loc = [None, None]
        v_loc = [None, None]
        kT_all = [None, None]
        v_all = [None, None]
        if 0 in phases:
            kT_loc[0] = dout("kT_loc0", [512, TOK], BF16)
            v_loc[0] = dout("v_loc0", [TOK, 512], BF16)
        if 1 in phases:
            kT_all[0] = din("kT_all0", [1024, TOK], BF16)
            v_all[0] = din("v_all0", [SEQ, 512], BF16)
            kT_loc[1] = dout("kT_loc1", [512, TOK], BF16)
            v_loc[1] = dout("v_loc1", [TOK, 512], BF16)
        if 2 in phases:
            kT_all[1] = din("kT_all1", [1024, TOK], BF16)
            v_all[1] = din("v_all1", [SEQ, 512], BF16)

    sb = {}
    ctxs = []

    def salloc(name, shape, dt):
        cm = nc.sbuf_tensor(name, list(shape), dt)
        t = cm.__enter__()
        ctxs.append(cm)
        sb[name] = t
        return t

    xT_t = salloc("xT", [128, KC * T], F32)
    hT_t = salloc("hT", [128, KC * T], BF16)
    NW = 5
    W_t = salloc("W", [128, NW * 4096], BF16)
    KV_t = salloc("KV", [128, 2 * 8192], BF16)
    R_t = salloc("R", [128, FC * T], BF16)
    rstd_t = salloc("rstdb", [128, T], F32)
    sq_t = salloc("sq", [128, 2 * T], BF16)
    silu_t = salloc("silu", [128, 2 * T], F32)
    recip_t = salloc("recip", [128, T], F32)
    lnt_t = salloc("lnt", [128, T], F32)
    memkv_t = salloc("memkv", [128, 2 * 2048], BF16)
    wspf_t = salloc("wspf", [128, TW], F32)
    bsp_t = salloc("bspt", [128, TW], F32)
    cols_t = salloc("colst", [128, NCOLS], F32)
    colsS_t = salloc("colsS", [128, NCOLS], F32)
    ones_t = salloc("ones", [128, 128], BF16)
    ident_t = salloc("identt", [128, 128], F32)
    rot_t = salloc("rott", [128, 128], BF16)
    small_t = salloc("small", [128, 64], F32)
    PS_cm = nc.psum_tensor("ps", [128, 8 * 512], F32)
    PS_t = PS_cm.__enter__()
    ctxs.append(PS_cm)

    xT = [Buf(xT_t[:, k * T:(k + 1) * T], [("xT", k)]) for k in range(KC)]
    hT = [Buf(hT_t[:, k * T:(k + 1) * T], [("hT", k)]) for k in range(KC)]
    Wb = [Buf(W_t[:, s * 4096:(s + 1) * 4096], [("W", s)]) for s in range(NW)]
    KTb = [Buf(KV_t[:, s * 8192:s * 8192 + 4096], [("KT", s)]) for s in range(2)]
    Vb = [Buf(KV_t[:, s * 8192 + 4096:(s + 1) * 8192], [("V", s)]) for s in range(2)]
    Ratom = lambda a0, n=1: [("R", a) for a in range(a0, a0 + n)]
    actT = [Buf(R_t[:, j * T:(j + 1) * T], Ratom(j)) for j in range(FC)]
    catT = actT[:16]
    vn = [Buf(R_t[:, (16 + 3 * tt) * T:(19 + 3 * tt) * T], Ratom(16 + 3 * tt, 3)) for tt in range(4)]
    qnb = [Buf(R_t[:, (16 + i) * T:(17 + i) * T], Ratom(16 + i)) for i in range(2)]
    PTb = [Buf(R_t[:, (18 + i) * T:(19 + i) * T], Ratom(18 + i)) for i in range(4)]
    PTb6 = PTb + [Buf(R_t[:, (42 + i) * T:(43 + i) * T], Ratom(42 + i)) for i in range(2)]

    def r32(a0):
        return Buf(R_t[:, a0 * T:(a0 + 2) * T].bitcast(F32), Ratom(a0, 2))

    cosb, sinb = r32(22), r32(24)
    tmpA, tmpB, tmpC, tmpD = r32(28), r32(30), r32(32), r32(34)
    kst = Buf(R_t[:, 36 * T:40 * T], Ratom(36, 4))
    vst = Buf(R_t[:, 40 * T:44 * T], Ratom(40, 4))
    xst = Buf(R_t[:, 0:8 * T].bitcast(F32), Ratom(0, 8))
    rstdb = Buf(rstd_t[:, :], [("rstdb", 0)])
    sqbufs = [Buf(sq_t[:, i * T:(i + 1) * T], [("sq", i)]) for i in range(2)]
    silub = [Buf(silu_t[:, i * T:(i + 1) * T], [("silu", i)]) for i in range(2)]
    recipb = Buf(recip_t[:, :], [("recip", 0)])
    lnt = Buf(lnt_t[:, :], [("lnt", 0)])
    KmT = [Buf(memkv_t[:, s * 2048:s * 2048 + 1024], [("KmT", s)]) for s in range(2)]
    Vm = [Buf(memkv_t[:, s * 2048 + 1024:(s + 1) * 2048], [("Vm", s)]) for s in range(2)]
    wspf = Buf(wspf_t[:, :], [("wspf", 0)])
    bspb = Buf(bsp_t[:, :], [("bsp", 0)])
    colsb = Buf(cols_t[:, :], [("cols", 0)])
    colsS = Buf(colsS_t[:, :], [("colsS", 0)])
    onesb = Buf(ones_t[:, :], [("ones", 0)])
    identb = Buf(ident_t[:, :], [("ident", 0)])
    rotb = Buf(rot_t[:, :], [("rot", 0)])
    smallb = Buf(small_t[:, :], [("small", 0)])
    PSb = [Buf(PS_t[:, b * 512:(b + 1) * 512], [("ps", b)]) for b in range(8)]

    st = {"ps": 0, "w": 0, "silu": 0, "qn": 0, "pt": 0, "kv": 0, "sq": 0, "pt6": 0, "acc": 0}

    reserved = set()

    def next_ps():
        while True:
            b = st["ps"]
            st["ps"] = (b + 1) % 8
            if b not in reserved:
                return PSb[b]

    def reserve_ps():
        p = next_ps()
        reserved.add(PSb.index(p))
        return p

    def release_ps(p):
        reserved.discard(PSb.index(p))

    def atoms(*bufs):
        out = []
        for b in bufs:
            out.extend(b.atoms)
        return out

    def mm(ps_ap, lhsT, rhs, start, stop, reads, writes):
        S.op("pe", lambda e: e.matmul(ps_ap, lhsT, rhs, start=start, stop=stop), reads=reads, writes=writes,
             lazy=(not stop) and LAZY_PE)

    def dma(q, out_ap, in_ap, reads, writes, sem):
        if sem not in S.sem_names:
            S.new_sem(sem)
        return S.op(q, lambda e: e.dma_start(out=out_ap, in_=in_ap), reads=reads, writes=writes, sem=sem, inc=16)

    def _small(ap, **kw):
        n = 1
        for d in ap.shape[1:]:
            n *= d
        return n < 64 or ("accum_out" in kw)

    def act(out_ap, in_ap, func, reads, writes, **kw):
        S.op("act", lambda e: e.activation(out_ap, in_ap, func, **kw), reads=reads, writes=writes,
             small=_small(out_ap, **kw))

    def dve(fn, reads, writes):
        S.op("dve", fn, reads=reads, writes=writes)

    def dve_tt(out, in0, in1, op, reads, writes):
        S.op("dve", lambda e: e.tensor_tensor(out, in0, in1, op), reads=reads, writes=writes, small=_small(out))

    def dve_ts(out, in0, s1, s2, op0, op1, reads, writes):
        if op1 is None:
            S.op("dve", lambda e: e.tensor_scalar(out, in0, s1, None, op0), reads=reads, writes=writes, small=_small(out))
        else:
            S.op("dve", lambda e: e.tensor_scalar(out, in0, s1, s2, op0, op1), reads=reads, writes=writes, small=_small(out))

    def dve_stt(out, in0, sc, in1, op0, op1, reads, writes):
        S.op("dve", lambda e: e.scalar_tensor_tensor(out, in0, sc, in1, op0, op1), reads=reads, writes=writes, small=_small(out))

    def pool_tt(out, in0, in1, op, reads, writes):
        S.op("pool", lambda e: e.tensor_tensor(out, in0, in1, op), reads=reads, writes=writes)

    def pool_copy(out, in_, reads, writes):
        S.op("pool", lambda e: e.tensor_copy(out, in_), reads=reads, writes=writes)

    def dve_copy(out, in_, reads, writes):
        S.op("dve", lambda e: e.tensor_copy(out, in_), reads=reads, writes=writes, small=_small(out))

    def dve_recip(out, in_, reads, writes):
        S.op("dve", lambda e: e.reciprocal(out, in_), reads=reads, writes=writes, small=_small(out))

    def dve_rsum(out, in_, reads, writes):
        S.op("dve", lambda e: e.reduce_sum(out, in_, AX.X), reads=reads, writes=writes, small=True)

    def pe_tr(out, in_, reads, writes):
        S.op("pe", lambda e: e.transpose(out, in_, identb.ap), reads=reads + identb.atoms, writes=writes)

    def wload(w2d, r0, nk, c0, ncols):
        s = st["w"]
        st["w"] = (s + 1) % NW
        wgen[s] = wgen.get(s, 0) + 1
        assert ncols == 256 and c0 % 256 == 0
        flat = Wb[s].ap[:, 0:nk * ncols]
        src = w2d[c0 // 256][:, r0 * 256:(r0 + nk) * 256]
        dma("pool", flat, src, reads=[], writes=Wb[s].atoms, sem=f"w{s}")
        return flat.rearrange("p (k c) -> p k c", k=nk), Wb[s]

    wgen = {}

    gcol = lambda c: colsb.ap[:, c:c + 1]
    gcolS = lambda c: colsS.ap[:, c:c + 1]

    dma("sp", colsb.ap, cols_in, [], colsb.atoms, "c0")
    dma("sp", identb.ap, ident_in, [], identb.atoms, "c1")
    dma("pool", rotb.ap, rot_in, [], rotb.atoms, "c2")
    S.op("dve", lambda e: e.memset(onesb.ap, 1.0), writes=onesb.atoms)
    ones32_t = salloc("ones32", [128, 128], F32)
    ones32 = Buf(ones32_t[:, :], [("ones32", 0)])
    S.op("dve", lambda e: e.memset(ones32.ap, 1.0), writes=ones32.atoms)
    epst = salloc("epst", [128, 4], F32)
    epsb = {}
    for i, n in enumerate((2048, 128, 1536)):
        S.op("dve", (lambda i=i, n=n: (lambda e: e.memset(epst[:, i:i + 1], float(n * EPS))))(), writes=[("eps", 0)], small=True)
        epsb[n] = epst[:, i:i + 1]
    for (c0, c1, n) in ((C_GMIX, C_GMQ, 2048.0), (C_GMQ, C_GV, 128.0), (C_GV, NCOLS, 1536.0)):
        dve_ts(colsS.ap[:, c0:c1], colsb.ap[:, c0:c1], float(np.sqrt(n)), None, ALU.mult, None,
               colsb.atoms, colsS.atoms)

    def rstd_from(out_ap, ss_ap, n, reads, writes):
        w = out_ap.shape[-1]
        act(lnt.ap[:, 0:w], ss_ap, AF.Ln, reads=list(reads) + [("eps", 0)], writes=lnt.atoms, bias=epsb[n], scale=1.0)
        act(out_ap, lnt.ap[:, 0:w], AF.Exp, reads=lnt.atoms, writes=writes, scale=-0.5)

    def next_sq():
        b = sqbufs[st["sq"]]
        st["sq"] ^= 1
        return b

    def mem_kv(l, slot):
        hm = hT
        for mt in range(2):
            dma("sp", xst.ap, mem_in[mt * 128:(mt + 1) * 128, :], [], xst.atoms, "xst")
            for q in range(4):
                act(tmpA.ap, xst.ap[:, q * 512:(q + 1) * 512], AF.Square,
                    reads=xst.atoms, writes=tmpA.atoms + smallb.atoms, accum_out=smallb.ap[:, q:q + 1])
            dve_rsum(smallb.ap[:, 4:5], smallb.ap[:, 0:4], smallb.atoms, smallb.atoms)
            rstd_from(smallb.ap[:, 5:6], smallb.ap[:, 4:5], 2048, smallb.atoms, smallb.atoms)
            dve_ts(xst.ap, xst.ap, smallb.ap[:, 5:6], None, ALU.mult, None, xst.atoms + smallb.atoms, xst.atoms)
            for k in range(KC):
                ps = next_ps()
                pe_tr(ps.ap[:, 0:128], xst.ap[:, k * 128:(k + 1) * 128], xst.atoms, ps.atoms)
                act(hm[k].ap[:, mt * 128:(mt + 1) * 128], ps.ap[:, 0:128], AF.Copy,
                    reads=ps.atoms + colsS.atoms, writes=hm[k].atoms, scale=gcolS(C_GMEM + l * 16 + k))
        wl = w_mem_kv[l]
        for half in range(2):
            wv, wb = wload(wl, 0, KC, half * 256, 256)
            for hh in range(2):
                h = half * 2 + hh
                ps = next_ps()
                for k in range(KC):
                    mm(ps.ap[:, 0:256], wv[:, k, hh * 128:(hh + 1) * 128], hm[k].ap[:, 0:256], k == 0, k == KC - 1,
                       reads=wb.atoms + hm[k].atoms, writes=ps.atoms)
                head_norm(ps, 256, C_GMK + l, KmT[slot].ap[:, h * 256:(h + 1) * 256], KmT[slot].atoms)
        pss = [next_ps(), next_ps()]
        for half in range(2):
            wv, wb = wload(wl, 0, KC, 512 + half * 256, 256)
            for mt in range(2):
                for k in range(KC):
                    mm(pss[mt].ap[:, half * 256:(half + 1) * 256], hm[k].ap[:, mt * 128:(mt + 1) * 128], wv[:, k, :],
                       k == 0, k == KC - 1, reads=wb.atoms + hm[k].atoms, writes=pss[mt].atoms)
        for mt in range(2):
            act(Vm[slot].ap[:, mt * 512:(mt + 1) * 512], pss[mt].ap, AF.Copy, reads=pss[mt].atoms, writes=Vm[slot].atoms)

    def norm_block(gbase):
        ps = next_ps()
        for k in range(KC):
            sq = next_sq()
            act(sq.ap, xT[k].ap, AF.Square, reads=xT[k].atoms, writes=sq.atoms)
            mm(ps.ap, onesb.ap, sq.ap, k == 0, k == KC - 1, reads=onesb.atoms + sq.atoms, writes=ps.atoms)
        rstd_from(rstdb.ap, ps.ap, 2048, ps.atoms, rstdb.atoms)
        for k in range(KC):
            dve_stt(hT[k].ap, xT[k].ap, gcolS(gbase + k), rstdb.ap, ALU.mult, ALU.mult,
                    xT[k].atoms + rstdb.atoms + colsS.atoms, hT[k].atoms)

    def head_norm(ps, n, gidx, out_ap, out_atoms):
        sq = next_sq()
        act(sq.ap[:, 0:n], ps.ap[:, 0:n], AF.Square, reads=ps.atoms, writes=sq.atoms)
        ps2 = next_ps()
        mm(ps2.ap[:, 0:n], onesb.ap, sq.ap[:, 0:n], True, True, reads=onesb.atoms + sq.atoms, writes=ps2.atoms)
        rstd_from(rstdb.ap[:, 0:n], ps2.ap[:, 0:n], 128, ps2.atoms, rstdb.atoms)
        dve_stt(out_ap, ps.ap[:, 0:n], gcolS(gidx), rstdb.ap[:, 0:n], ALU.mult, ALU.mult,
                ps.atoms + rstdb.atoms + colsS.atoms, out_atoms)

    def rope(qb, out_ap, out_atoms):
        ps = next_ps()
        mm(ps.ap, rotb.ap, qb.ap, True, True, reads=rotb.atoms + qb.atoms, writes=ps.atoms)
        dve_tt(tmpA.ap, qb.ap, cosb.ap, ALU.mult, qb.atoms + cosb.atoms, tmpA.atoms)
        dve_tt(tmpB.ap, ps.ap, sinb.ap, ALU.mult, ps.atoms + sinb.atoms, tmpB.atoms)
        dve_tt(out_ap, tmpA.ap, tmpB.ap, ALU.add, tmpA.atoms + tmpB.atoms, out_atoms)

    def proj_norm4(wl, c0, gidx, rst, sqs, outs, rope_to=None):
        psq = []
        for half in range(2):
            wv, wb = wload(wl, 0, KC, c0 + half * 256, 256)
            for hh in range(2):
                ps = next_ps()
                psq.append(ps)
                for k in range(KC):
                    mm(ps.ap, wv[:, k, hh * 128:(hh + 1) * 128], hT[k].ap, k == 0, k == KC - 1,
                       reads=wb.atoms + hT[k].atoms, writes=ps.atoms)
        for h in range(4):
            act(sqs[h].ap, psq[h].ap, AF.Square, reads=psq[h].atoms, writes=sqs[h].atoms)
        ps2 = []
        for h in range(4):
            p = next_ps()
            ps2.append(p)
            mm(p.ap, onesb.ap, sqs[h].ap, True, True, reads=onesb.atoms + sqs[h].atoms, writes=p.atoms)
        for h in range(4):
            rstd_from(rst[h].ap, ps2[h].ap, 128, ps2[h].atoms, rst[h].atoms)
        for h in range(4):
            dve_stt(outs[h].ap, psq[h].ap, gcolS(gidx), rst[h].ap, ALU.mult, ALU.mult,
                    psq[h].atoms + rst[h].atoms + colsS.atoms, outs[h].atoms)
        if rope_to is not None:
            for h in range(4):
                mm(ps2[h].ap, rotb.ap, outs[h].ap, True, True, reads=rotb.atoms + outs[h].atoms, writes=ps2[h].atoms)
            for h in range(4):
                o_ap, o_atoms = rope_to[h]
                dve_tt(tmpC.ap, outs[h].ap, cosb.ap, ALU.mult, outs[h].atoms + cosb.atoms, tmpC.atoms)
                dve_tt(tmpD.ap, ps2[h].ap, sinb.ap, ALU.mult, ps2[h].atoms + sinb.atoms, tmpD.atoms)
                dve_tt(o_ap, tmpC.ap, tmpD.ap, ALU.add, tmpC.atoms + tmpD.atoms, o_atoms)

    def mem_core(slot, qbufs):
        def mem_S(h):
            pts = []
            for c in range(2):
                pss = next_ps()
                mm(pss.ap, KmT[slot].ap[:, h * 256 + c * 128:h * 256 + (c + 1) * 128], qbufs[h].ap, True, True,
                   reads=KmT[slot].atoms + qbufs[h].atoms, writes=pss.atoms)
                pt = next_pt()
                act(pt.ap, pss.ap, AF.Exp, reads=pss.atoms, writes=pt.atoms, scale=float(SCALE))
                pts.append(pt)
            return pts

        nxt = mem_S(0)
        for h in range(4):
            pts = nxt
            pso2, psm2 = next_ps(), next_ps()
            if h + 1 < 4:
                reserved.update({PSb.index(pso2), PSb.index(psm2)})
                nxt = mem_S(h + 1)
                reserved.difference_update({PSb.index(pso2), PSb.index(psm2)})
            for c in range(2):
                mm(pso2.ap, Vm[slot].ap[:, c * 512 + h * 128:c * 512 + (h + 1) * 128], pts[c].ap, c == 0, c == 1,
                   reads=Vm[slot].atoms + pts[c].atoms, writes=pso2.atoms)
            for c in range(2):
                mm(psm2.ap, onesb.ap, pts[c].ap, c == 0, c == 1, reads=onesb.atoms + pts[c].atoms, writes=psm2.atoms)
            dve_recip(recipb.ap, psm2.ap, psm2.atoms, recipb.atoms)
            dve_tt(catT[12 + h].ap, pso2.ap, recipb.ap, ALU.mult, pso2.atoms + recipb.atoms, catT[12 + h].atoms)

    def next_qn():
        b = qnb[st["qn"]]
        st["qn"] ^= 1
        return b

    def next_pt():
        b = PTb[st["pt"] % 4]
        st["pt"] = (st["pt"] + 1) % 4
        return b

    def next_pt6():
        b = PTb6[st["pt6"]]
        st["pt6"] = (st["pt6"] + 1) % 6
        return b

    def mem_attention(l, slot, wl, qc0):
        for half in range(2):
            wv, wb = wload(wl, 0, KC, qc0 + half * 256, 256)
            for hh in range(2):
                h = half * 2 + hh
                ps = next_ps()
                for k in range(KC):
                    mm(ps.ap, wv[:, k, hh * 128:(hh + 1) * 128], hT[k].ap, k == 0, k == KC - 1,
                       reads=wb.atoms + hT[k].atoms, writes=ps.atoms)
                qb = next_qn()
                head_norm(ps, T, C_GMQ + l, qb.ap, qb.atoms)
                pts = []
                for c in range(2):
                    pss = next_ps()
                    mm(pss.ap, KmT[slot].ap[:, h * 256 + c * 128:h * 256 + (c + 1) * 128], qb.ap, True, True,
                       reads=KmT[slot].atoms + qb.atoms, writes=pss.atoms)
                    pt = next_pt()
                    act(pt.ap, pss.ap, AF.Exp, reads=pss.atoms, writes=pt.atoms, scale=float(SCALE))
                    pts.append(pt)
                pso, psm = next_ps(), next_ps()
                for c in range(2):
                    mm(pso.ap, Vm[slot].ap[:, c * 512 + h * 128:c * 512 + (h + 1) * 128], pts[c].ap, c == 0, c == 1,
                       reads=Vm[slot].atoms + pts[c].atoms, writes=pso.atoms)
                for c in range(2):
                    mm(psm.ap, onesb.ap, pts[c].ap, c == 0, c == 1, reads=onesb.atoms + pts[c].atoms, writes=psm.atoms)
                dve_recip(recipb.ap, psm.ap, psm.atoms, recipb.atoms)
                dve_tt(catT[12 + h].ap, pso.ap, recipb.ap, ALU.mult, pso.atoms + recipb.atoms, catT[12 + h].atoms)

    def out_proj(l):
        wl = w_out[l]
        for mp in range(8):
            wv, wb = wload(wl, 0, KC, mp * 256, 256)
            for mi in range(2):
                m = mp * 2 + mi
                ps = next_ps()
                for c in range(KC):
                    mm(ps.ap, wv[:, c, mi * 128:(mi + 1) * 128], catT[c].ap, c == 0, c == KC - 1,
                       reads=wb.atoms + catT[c].atoms, writes=ps.atoms)
                dve_tt(xT[m].ap, xT[m].ap, ps.ap, ALU.add, xT[m].atoms + ps.atoms, xT[m].atoms)

    def ffn(l):
        norm_block(C_GFFN + l * 16)
        wl = w_gu[l]
        for jp in range(FC // 2):
            wg, wgb = wload(wl, 0, KC, jp * 256, 256)
            wu, wub = wload(wl, 0, KC, DFF + jp * 256, 256)
            for ji in range(2):
                j = jp * 2 + ji
                psg, psu = next_ps(), next_ps()
                for k in range(KC):
                    mm(psg.ap, wg[:, k, ji * 128:(ji + 1) * 128], hT[k].ap, k == 0, k == KC - 1,
                       reads=wgb.atoms + hT[k].atoms, writes=psg.atoms)
                for k in range(KC):
                    mm(psu.ap, wu[:, k, ji * 128:(ji + 1) * 128], hT[k].ap, k == 0, k == KC - 1,
                       reads=wub.atoms + hT[k].atoms, writes=psu.atoms)
                sb_ = silub[st["silu"]]
                st["silu"] ^= 1
                act(sb_.ap, psg.ap, AF.Silu, reads=psg.atoms, writes=sb_.atoms)
                dve_tt(actT[j].ap, sb_.ap, psu.ap, ALU.mult, sb_.atoms + psu.atoms, actT[j].atoms)
        wl = w_dn[l]
        fsplit = [(0, 16), (16, 16), (32, 12)]
        for mp in range(8):
            pss = [next_ps(), next_ps()]
            for (f0, nf) in fsplit:
                wv, wb = wload(wl, f0, nf, mp * 256, 256)
                for mi in range(2):
                    for f in range(nf):
                        fg = f0 + f
                        mm(pss[mi].ap, wv[:, f, mi * 128:(mi + 1) * 128], actT[fg].ap, fg == 0, fg == FC - 1,
                           reads=wb.atoms + actT[fg].atoms, writes=pss[mi].atoms)
            for mi in range(2):
                m = mp * 2 + mi
                dve_tt(xT[m].ap, xT[m].ap, pss[mi].ap, ALU.add, xT[m].atoms + pss[mi].atoms, xT[m].atoms)

    wsc_all = [Buf(R_t[:, (32 + 3 * tt) * T:(35 + 3 * tt) * T], Ratom(32 + 3 * tt, 3)) for tt in range(4)]

    def layer_a_mixer(l, slot):
        ia = l // 2
        wl = w_in_a[ia]
        norm_block(C_GMIX + l * 16)
        dma("sp", wspf.ap, w_spT[ia], [], wspf.atoms, "wspf")
        dma("sp", bspb.ap, bsp_in[ia], [], bspb.atoms, "bsp")
        for wt in range(6):
            wv, wb = wload(wl, 0, KC, TW + wt * 256, 256)
            for tt in range(4):
                ps = next_ps()
                for k in range(KC):
                    mm(ps.ap[:, 0:256], hT[k].ap[:, tt * 128:(tt + 1) * 128], wv[:, k, :], k == 0, k == KC - 1,
                       reads=wb.atoms + hT[k].atoms, writes=ps.atoms)
                act(tmpA.ap[:, 0:256], ps.ap[:, 0:256], AF.Gelu, reads=ps.atoms, writes=tmpA.atoms)
                act(tmpB.ap[:, 0:256], tmpA.ap[:, 0:256], AF.Square, reads=tmpA.atoms, writes=tmpB.atoms + smallb.atoms,
                    accum_out=smallb.ap[:, 8 + tt * 6 + wt:9 + tt * 6 + wt])
                dve_copy(vn[tt].ap[:, wt * 256:(wt + 1) * 256], tmpA.ap[:, 0:256], tmpA.atoms, vn[tt].atoms)
        dve_rsum(smallb.ap[:, 32:36], smallb.ap[:, 8:32].rearrange("p (a b) -> p a b", a=4), smallb.atoms, smallb.atoms)
        rstd_from(smallb.ap[:, 36:40], smallb.ap[:, 32:36], 1536, smallb.atoms, smallb.atoms)
        for tt in range(4):
            dve_ts(wsc_all[tt].ap, wspf.ap, smallb.ap[:, 36 + tt:37 + tt], None, ALU.mult, None,
                   wspf.atoms + smallb.atoms, wsc_all[tt].atoms)
        for up in range(6):
            wv, wb = wload(wl, 0, KC, up * 256, 256)
            for gi in range(2):
                g = up * 2 + gi
                psu = next_ps()
                for k in range(KC):
                    mm(psu.ap, wv[:, k, gi * 128:(gi + 1) * 128], hT[k].ap, k == 0, k == KC - 1,
                       reads=wb.atoms + hT[k].atoms, writes=psu.atoms)
                act(tmpA.ap, psu.ap, AF.Gelu, reads=psu.atoms, writes=tmpA.atoms)
                pss = next_ps()
                for tt in range(4):
                    mm(pss.ap[:, tt * 128:(tt + 1) * 128], vn[tt].ap[:, g * 128:(g + 1) * 128],
                       wsc_all[tt].ap[:, g * 128:(g + 1) * 128], True, True,
                       reads=vn[tt].atoms + wsc_all[tt].atoms, writes=pss.atoms)
                dve_stt(tmpB.ap.rearrange("p (a b) -> p a b", a=4), pss.ap.rearrange("p (a b) -> p a b", a=4),
                        gcolS(C_GV + ia * 12 + g),
                        bspb.ap[:, g * 128:(g + 1) * 128].unsqueeze(1).broadcast_to([128, 4, 128]),
                        ALU.mult, ALU.add, pss.atoms + colsS.atoms + bspb.atoms, tmpB.atoms)
                dve_tt(catT[g].ap, tmpA.ap, tmpB.ap, ALU.mult, tmpA.atoms + tmpB.atoms, catT[g].atoms)
        qbufs = [Buf(R_t[:, (22 + h) * T:(23 + h) * T], Ratom(22 + h)) for h in range(4)]
        proj_norm4(wl, 2 * TW, C_GMQ + l, [r32(32), r32(34), r32(36), r32(38)], PTb, qbufs)
        mem_core(slot, qbufs)

    def load_rope_tables(blk):
        dma("sp", cosb.ap, cos_in[:, blk * T:(blk + 1) * T], [], cosb.atoms, "cos")
        dma("sp", sinb.ap, sin_in[:, blk * T:(blk + 1) * T], [], sinb.atoms, "sin")

    def layer_b_kv(l, blk, e):
        ib = l // 2
        wl = w_in_b[ib]
        norm_block(C_GMIX + l * 16)
        load_rope_tables(blk)
        kq = [Buf(R_t[:, (8 + h) * T:(9 + h) * T], Ratom(8 + h)) for h in range(4)]
        proj_norm4(wl, TW, C_GK + ib, [r32(0), r32(2), r32(4), r32(6)], PTb, kq,
                   rope_to=[(kst.ap[:, h * T:(h + 1) * T], kst.atoms) for h in range(4)])
        dma("sp", kT_loc[e].rearrange("(h d) t -> d h t", d=128)[:, :, blk * T:(blk + 1) * T],
            kst.ap.rearrange("p (h t) -> p h t", h=4), kst.atoms, [("kTloc", e, blk)], "kst")
        pss = [next_ps() for _ in range(4)]
        for half in range(2):
            wv, wb = wload(wl, 0, KC, TW + 512 + half * 256, 256)
            for tt in range(4):
                for k in range(KC):
                    mm(pss[tt].ap[:, half * 256:(half + 1) * 256], hT[k].ap[:, tt * 128:(tt + 1) * 128], wv[:, k, :],
                       k == 0, k == KC - 1, reads=wb.atoms + hT[k].atoms, writes=pss[tt].atoms)
        for tt in range(4):
            act(vst.ap[:, tt * T:(tt + 1) * T], pss[tt].ap, AF.Copy, reads=pss[tt].atoms, writes=vst.atoms)
        dma("sp", v_loc[e][blk * T:(blk + 1) * T, :].rearrange("(tt p) c -> p tt c", p=128),
            vst.ap.rearrange("p (tt c) -> p tt c", tt=4), vst.atoms, [("vloc", e, blk)], "vst")

    qrb = [Buf(tmpC.ap.bitcast(BF16)[:, 0:T], tmpC.atoms), Buf(tmpD.ap.bitcast(BF16)[:, 0:T], tmpD.atoms)]

    qmnb = [Buf(R_t[:, (36 + i) * T:(37 + i) * T], Ratom(36 + i)) for i in range(4)]
    accb = [r32(26), r32(40)]

    def layer_b_attn(l, blk, e, slot):
        ib = l // 2
        wl = w_in_b[ib]
        norm_block(C_GMIX + l * 16)
        load_rope_tables(blk)
        prepA, prepB = reserve_ps(), reserve_ps()
        wq = {}

        def qtile(c0):
            cb = (c0 // 256) * 256
            if cb in wq:
                wv, wb, g = wq[cb]
                if wgen[Wb.index(wb)] != g:
                    del wq[cb]
            if cb not in wq:
                wv, wb = wload(wl, 0, KC, cb, 256)
                wq[cb] = (wv, wb, wgen[Wb.index(wb)])
            wv, wb, _ = wq[cb]
            return wv, wb, c0 - cb

        def prep_gen(c0, gidx, out_buf, do_rope):
            wv, wb, off = qtile(c0)
            for k in range(KC):
                mm(prepA.ap, wv[:, k, off:off + 128], hT[k].ap, k == 0, k == KC - 1,
                   reads=wb.atoms + hT[k].atoms, writes=prepA.atoms)
                if k % 2 == 1:
                    yield
            sq = next_sq()
            act(sq.ap, prepA.ap, AF.Square, reads=prepA.atoms, writes=sq.atoms)
            mm(prepB.ap, onesb.ap, sq.ap, True, True, reads=onesb.atoms + sq.atoms, writes=prepB.atoms)
            yield
            rstd_from(rstdb.ap, prepB.ap, 128, prepB.atoms, rstdb.atoms)
            if do_rope:
                qb = next_qn()
                dve_stt(qb.ap, prepA.ap, gcolS(gidx), rstdb.ap, ALU.mult, ALU.mult,
                        prepA.atoms + rstdb.atoms + colsS.atoms, qb.atoms)
                mm(prepB.ap, rotb.ap, qb.ap, True, True, reads=rotb.atoms + qb.atoms, writes=prepB.atoms)
                yield
                dve_tt(tmpA.ap, qb.ap, cosb.ap, ALU.mult, qb.atoms + cosb.atoms, tmpA.atoms)
                dve_tt(tmpB.ap, prepB.ap, sinb.ap, ALU.mult, prepB.atoms + sinb.atoms, tmpB.atoms)
                dve_tt(out_buf.ap, tmpA.ap, tmpB.ap, ALU.add, tmpA.atoms + tmpB.atoms, out_buf.atoms)
            else:
                dve_stt(out_buf.ap, prepA.ap, gcolS(gidx), rstdb.ap, ALU.mult, ALU.mult,
                        prepA.atoms + rstdb.atoms + colsS.atoms, out_buf.atoms)
            yield

        def run_all(g):
            for _ in g:
                pass

        def q_gen(qh):
            return prep_gen(qh * 128, C_GQ + ib, qrb[qh % 2], True)

        def m_gen(h):
            return prep_gen(TW + 1024 + h * 128, C_GMQ + l, qmnb[h], False)

        qtile(0)
        run_all(q_gen(0))
        psos, psm = [reserve_ps(), reserve_ps()], reserve_ps()
        LOOK = 2
        for kvh in range(4):
            s = st["kv"]
            st["kv"] ^= 1
            dma("sp", KTb[s].ap.rearrange("p (r t) -> p r t", r=2),
                kT_all[e].rearrange("(r h d) t -> h d r t", r=2, h=4)[kvh],
                [("kTall", e)], KTb[s].atoms, f"kt{s}")
            for jq in range(4):
                dma("sp", Vb[s].ap.rearrange("p (j c) -> p j c", j=32)[:, jq * 8:(jq + 1) * 8, :],
                    v_all[e].rearrange("(j p) c -> p j c", p=128)[:, jq * 8:(jq + 1) * 8, kvh * 128:(kvh + 1) * 128],
                    [("vall", e)], Vb[s].atoms, f"vv{s}")
            for qi in range(3):
                qh = kvh * 3 + qi
                qr = qrb[qh % 2]
                pso = psos[qh % 2]
                if qh + 2 < 12:
                    qtile((qh + 2) * 128)
                if 3 <= qh < 7:
                    qtile(TW + 1024 + (qh - 3) * 128)
                gens = []
                sched = {}
                if qh + 1 < 12:
                    g = q_gen(qh + 1)
                    gens.append(g)
                    for jj in list(range(1, 9)) + [10, 14, 18]:
                        sched[jj] = g
                if 4 <= qh < 8:
                    g = m_gen(qh - 4)
                    gens.append(g)
                    for jj in list(range(19, 27)) + [28, 30]:
                        sched[jj] = g

                acc = accb[st["acc"]]
                accp = silub[st["acc"]]
                st["acc"] ^= 1

                def emit_S(j):
                    pss = next_ps()
                    mm(pss.ap, KTb[s].ap[:, j * 128:(j + 1) * 128], qr.ap, True, True,
                       reads=KTb[s].atoms + qr.atoms, writes=pss.atoms)
                    pt = next_pt6()
                    act(pt.ap, pss.ap, AF.Exp, reads=pss.atoms, writes=pt.atoms, scale=float(SCALE))
                    a_ = acc if j % 2 == 0 else accp
                    if j < 2:
                        dve_copy(a_.ap, pt.ap, pt.atoms, a_.atoms)
                    else:
                        dve_tt(a_.ap, a_.ap, pt.ap, ALU.add, a_.atoms + pt.atoms, a_.atoms)
                    return pt

                pts = {}
                for j in range(LOOK):
                    pts[j] = emit_S(j)
                for j in range(32):
                    if j + LOOK < 32:
                        pts[j + LOOK] = emit_S(j + LOOK)
                    pt = pts.pop(j)
                    mm(pso.ap, Vb[s].ap[:, j * 128:(j + 1) * 128], pt.ap, j == 0, j == 31,
                       reads=Vb[s].atoms + pt.atoms, writes=pso.atoms)
                    if j in sched:
                        next(sched[j], None)
                for g in gens:
                    run_all(g)
                dve_tt(acc.ap, acc.ap, accp.ap, ALU.add, acc.atoms + accp.atoms, acc.atoms)
                mm(psm.ap, ones32.ap, acc.ap, True, True, reads=ones32.atoms + acc.atoms, writes=psm.atoms)
                act(lnt.ap, psm.ap, AF.Ln, reads=psm.atoms, writes=lnt.atoms)
                act(recipb.ap, lnt.ap, AF.Exp, reads=lnt.atoms, writes=recipb.atoms, scale=-1.0)
                dve_tt(catT[qh].ap, pso.ap, recipb.ap, ALU.mult, pso.atoms + recipb.atoms, catT[qh].atoms)
        release_ps(prepA)
        release_ps(prepB)
        release_ps(psos[0])
        release_ps(psos[1])
        release_ps(psm)
        mem_core(slot, qmnb)

    def load_x_tokenmajor(blk):
        for k in range(KC):
            dma("sp", xT[k].ap, x_in[blk][:, k * T:(k + 1) * T], [], xT[k].atoms, f"xl{k}")

    def store_x_tokenmajor(blk):
        for k in range(KC):
            dma("sp", y_out[blk][:, k * T:(k + 1) * T], xT[k].ap, xT[k].atoms, [("yout", blk, k)], f"xs{k}")

    def load_x_park(blk):
        for k in range(KC):
            dma("sp", xT[k].ap, xpark_r[blk][:, k * T:(k + 1) * T], [("xpark", blk, k)], xT[k].atoms, f"xl{k}")

    def store_x_park(blk):
        for k in range(KC):
            dma("sp", xpark_w[blk][:, k * T:(k + 1) * T], xT[k].ap, xT[k].atoms, [("xpark", blk, k)], f"xs{k}")

    def exchange(e):
        groups = [[0, 1], [2, 3], [4, 5], [6, 7]]
        rk = [("kTloc", e, b) for b in range(NBLK)]
        rv = [("vloc", e, b) for b in range(NBLK)]
        S.new_sem(f"cck{e}")
        S.new_sem(f"ccv{e}")
        S.op("pool", lambda en: en.collective_compute("AllGather", ALU.bypass, replica_groups=groups,
                                                      ins=[kT_loc[e]], outs=[kT_all[e]]),
             reads=rk, writes=[("kTall", e)], sem=f"cck{e}", inc=CC_INC)
        S.op("pool", lambda en: en.collective_compute("AllGather", ALU.bypass, replica_groups=groups,
                                                      ins=[v_loc[e]], outs=[v_all[e]]),
             reads=rv, writes=[("vall", e)], sem=f"ccv{e}", inc=CC_INC)

    def a_layer(l, slot):
        layer_a_mixer(l, slot)
        out_proj(l)
        ffn(l)

    for ph in phases:
        mem_layers = {0: [0], 1: [1, 2], 2: [3]}[ph]
        slots = {}
        for i, l in enumerate(mem_layers):
            mem_kv(l, i)
            slots[l] = i
        for blk in range(NBLK):
            if ph == 0:
                load_x_tokenmajor(blk)
                a_layer(0, slots[0])
                layer_b_kv(1, blk, 0)
                store_x_park(blk)
            elif ph == 1:
                load_x_park(blk)
                layer_b_attn(1, blk, 0, slots[1])
                out_proj(1)
                ffn(1)
                a_layer(2, slots[2])
                layer_b_kv(3, blk, 1)
                store_x_park(blk)
            else:
                load_x_park(blk)
                layer_b_attn(3, blk, 1, slots[3])
                out_proj(3)
                ffn(3)
                store_x_tokenmajor(blk)
        if fused and ph < 2:
            exchange(ph)

    out_keys = []
    if 2 in phases:
        out_keys += [("yout", b, k) for b in range(NBLK) for k in range(KC)]
    if not fused:
        if last_phase < 2:
            out_keys += [("xpark", b, k) for b in range(NBLK) for k in range(KC)]
            e = last_phase
            out_keys += [("kTloc", e, b) for b in range(NBLK)] + [("vloc", e, b) for b in range(NBLK)]
    S.wait_all("sp", out_keys)

    sem_cms = {name: nc.semaphore(name) for name in S.sem_names}
    sems = {}
    for name, cm in sem_cms.items():
        sems[name] = cm.__enter__()
        ctxs.append(cm)

    def replay(e, items):
        for it in items:
            if it[0] == "wait":
                e.wait_ge(sems[it[1]], it[2])
            else:
                _, fn, semname, inc = it
                ins = fn(e)
                if inc:
                    ins.then_inc(sems[semname], inc)

    with nc.Block() as block:
        @block.tensor
        def _(e):
            replay(e, S.lists["pe"])

        @block.scalar
        def _(e):
            replay(e, S.lists["act"])

        @block.vector
        def _(e):
            replay(e, S.lists["dve"])

        @block.gpsimd
        def _(e):
            replay(e, S.lists["pool"])

        @block.sync
        def _(e):
            replay(e, S.lists["sp"])

    for cm in reversed(ctxs):
        cm.__exit__(None, None, None)
    return nc, ({k: len(v) for k, v in S.lists.items()} if not Sched.TRACE else S.pe_labels)


CC_INC = 1
LAZY_PE = True


def _slice_weights(com, phases):
    Ls, As, Bs = _weight_sets(phases)
    d = dict(com)
    d["w_in_a"] = np.ascontiguousarray(com["w_in_a"][As]) if As else np.ascontiguousarray(com["w_in_a"][:1])
    d["w_in_b"] = np.ascontiguousarray(com["w_in_b"][Bs]) if Bs else np.ascontiguousarray(com["w_in_b"][:1])
    for k in ("w_mem_kv", "w_out", "w_gate_up", "w_down"):
        d[k] = np.ascontiguousarray(com[k][Ls])
    return d


def _common_inputs(inputs):
    f = lambda a: np.ascontiguousarray(np.asarray(a, dtype=np.float32))
    cols = np.zeros((128, NCOLS), np.float32)

    def colize(v):
        return np.asarray(v, np.float32).reshape(-1, 128).T

    for l in range(4):
        cols[:, C_GMIX + l * 16:C_GMIX + (l + 1) * 16] = colize(inputs["g_mix"][l])
        cols[:, C_GFFN + l * 16:C_GFFN + (l + 1) * 16] = colize(inputs["g_ffn"][l])
        cols[:, C_GMEM + l * 16:C_GMEM + (l + 1) * 16] = colize(inputs["g_mem"][l])
        cols[:, C_GMQ + l] = np.asarray(inputs["g_mq"][l], np.float32)
        cols[:, C_GMK + l] = np.asarray(inputs["g_mk"][l], np.float32)
    for i in range(2):
        cols[:, C_GQ + i] = np.asarray(inputs["g_q_b"][i], np.float32)
        cols[:, C_GK + i] = np.asarray(inputs["g_k_b"][i], np.float32)
        cols[:, C_GV + i * 12:C_GV + (i + 1) * 12] = colize(inputs["g_v_a"][i])
    ident, rot = _host_consts()
    wsp = np.asarray(inputs["w_spatial"], np.float32)
    w_spT = np.ascontiguousarray(wsp.transpose(0, 3, 1, 2).reshape(2, 128, TW))
    bsp = np.ascontiguousarray(np.broadcast_to(np.asarray(inputs["b_spatial"], np.float32).reshape(2, 1, TW), (2, 128, TW)))
    def tile(w):
        w = np.asarray(w, np.float32)
        L, R, C = w.shape
        nk = R // 128
        return np.ascontiguousarray(w.reshape(L, nk, 128, C // 256, 256).transpose(0, 3, 2, 1, 4)).reshape(L, C // 256, 128, nk * 256)

    com = {
        "cols": cols, "ident": ident, "rot": rot, "w_spT": w_spT, "bsp": bsp,
        "w_in_a": tile(inputs["w_in_a"]), "w_in_b": tile(inputs["w_in_b"]), "w_mem_kv": tile(inputs["w_mem_kv"]),
        "w_out": tile(inputs["w_out"]), "w_gate_up": tile(inputs["w_gate_up"]), "w_down": tile(inputs["w_down"]),
    }
    return com


_CACHE = {}


def _get_prog(phases, fused):
    key = (tuple(phases), fused)
    if key not in _CACHE:
        _CACHE[key] = build(list(phases), fused)[0]
    return _CACHE[key]


def _to_blocks(xs):
    return np.ascontiguousarray(xs.reshape(NBLK, T, KC, 128).transpose(0, 3, 2, 1)).reshape(NBLK, 128, KC * T)


def _from_blocks(yb):
    return yb.reshape(NBLK, 128, KC, T).transpose(0, 3, 2, 1).reshape(TOK, D)


def kernel(**inputs):
    x = np.asarray(inputs["x"], np.float32)
    mem = np.asarray(inputs["mem"], np.float32)
    com = _common_inputs(inputs)
    cosT, sinT = _rope_tables()
    per_core = []
    for c in range(8):
        b, half = c // 2, c % 2
        d = dict(com)
        d["mem"] = np.ascontiguousarray(mem[b])
        d["cosT"] = np.ascontiguousarray(cosT[:, half * TOK:(half + 1) * TOK])
        d["sinT"] = np.ascontiguousarray(sinT[:, half * TOK:(half + 1) * TOK])
        per_core.append(d)
    out = np.empty((4, SEQ, D), np.float32)
    if FUSED:
        nc = _get_prog((0, 1, 2), True)
        in_maps = []
        for c in range(8):
            b, half = c // 2, c % 2
            d = dict(per_core[c])
            d["x_in"] = _to_blocks(x[b, half * TOK:(half + 1) * TOK])
            in_maps.append(d)
        res = run_bass_kernel_spmd(nc, in_maps, core_ids=list(range(8)))
        for c in range(8):
            b, half = c // 2, c % 2
            out[b, half * TOK:(half + 1) * TOK] = _from_blocks(np.asarray(res.results[c]["y_out"]))
        return out
    state = [dict() for _ in range(8)]
    for ph in range(3):
        nc = _get_prog((ph,), False)
        wsl = _slice_weights(com, (ph,))
        in_maps = []
        for c in range(8):
            b, half = c // 2, c % 2
            d = dict(per_core[c])
            d.update({k: wsl[k] for k in ("w_in_a", "w_in_b", "w_mem_kv", "w_out", "w_gate_up", "w_down")})
            if ph == 0:
                d["x_in"] = _to_blocks(x[b, half * TOK:(half + 1) * TOK])
            else:
                d["xpark_in"] = state[c]["xpark"]
                e = ph - 1
                p0, p1 = state[2 * b], state[2 * b + 1]
                d[f"kT_all{e}"] = np.concatenate([p0["kT"], p1["kT"]], axis=0)
                d[f"v_all{e}"] = np.concatenate([p0["v"], p1["v"]], axis=0)
            in_maps.append(d)
        res = run_bass_kernel_spmd(nc, in_maps, core_ids=list(range(8)))
        new_state = [dict() for _ in range(8)]
        for c in range(8):
            r = res.results[c]
            if ph < 2:
                new_state[c]["xpark"] = r["xpark_out"]
                new_state[c]["kT"] = r[f"kT_loc{ph}"]
                new_state[c]["v"] = r[f"v_loc{ph}"]
            else:
                b, half = c // 2, c % 2
                out[b, half * TOK:(half + 1) * TOK] = _from_blocks(np.asarray(r["y_out"]))
        state = new_state
    return out
```

```python
import numpy as np
import concourse.bass as bass
import concourse.mybir as mybir
from concourse.bass_utils import run_bass_kernel_spmd

F32 = mybir.dt.float32
BF16 = mybir.dt.bfloat16
AF = mybir.ActivationFunctionType
ALU = mybir.AluOpType
AX = mybir.AxisListType

D = 2048
KC = 16
T = 512
NBLK = 4
TOK = 2048
SEQ = 4096
DFF = 5632
FC = 44
TW = 1536
HD = 128
EPS = 1e-6
N_MEM = 256
SCALE = HD ** -0.5

C_GMIX, C_GFFN, C_GMEM, C_GMQ, C_GMK, C_GQ, C_GK, C_GV, NCOLS = 0, 64, 128, 192, 196, 200, 202, 204, 228

FUSED = True


class Tok:
    __slots__ = ("sem", "v", "entry", "small")

    def __init__(self, sem, v, entry=None, small=False):
        self.sem, self.v, self.entry, self.small = sem, v, entry, small


class Sched:
    ENG = ("pe", "act", "dve", "pool", "sp")
    TRACE = False

    def __init__(self):
        self.lists = {e: [] for e in self.ENG}
        self.cnt = {}
        self.waited = {}
        self.last_w = {}
        self.readers = {}
        self.sem_names = list(self.ENG)
        self.pe_labels = []
        self.pending = {e: [] for e in self.ENG}

    def new_sem(self, name):
        assert name not in self.sem_names
        self.sem_names.append(name)
        return name

    def _resolve(self, tok):
        if tok.v is not None:
            return
        pend = self.pending[tok.sem]
        i = pend.index(tok)
        tok.entry[3] = 1
        v = self.cnt.get(tok.sem, 0) + 1
        self.cnt[tok.sem] = v
        for t in pend[:i + 1]:
            t.v = v
        del pend[:i + 1]

    def _collect(self, reads, writes):
        toks = []
        for k in reads:
            t = self.last_w.get(k)
            if t is not None:
                toks.append(t)
        for k in writes:
            t = self.last_w.get(k)
            if t is not None:
                toks.append(t)
            toks.extend(self.readers.get(k, ()))
        return toks

    def _emit_waits(self, eng, toks, skip_same=True):
        deps = {}
        for t in toks:
            if skip_same and t.sem == eng:
                if eng == "pe":
                    continue
                if t.v is not None and self.cnt.get(eng, 0) - t.v >= 4:
                    continue
            self._resolve(t)
            if deps.get(t.sem, 0) < t.v:
                deps[t.sem] = t.v
        for s_, v in deps.items():
            if self.waited.get((eng, s_), 0) >= v:
                continue
            self.waited[(eng, s_)] = v
            self.lists[eng].append(("wait", s_, v))

    def op(self, eng, fn, reads=(), writes=(), sem=None, inc=None, lazy=False, small=False):
        semname = sem or eng
        if inc is None:
            inc = 1 if sem is None else 16
        self._emit_waits(eng, self._collect(reads, writes))
        if lazy:
            entry = ["op", fn, semname, 0]
            tok = Tok(semname, None, entry)
            self.pending[semname].append(tok)
        else:
            v = self.cnt.get(semname, 0) + inc
            self.cnt[semname] = v
            entry = ["op", fn, semname, inc]
            tok = Tok(semname, v, small=small)
            if semname in self.pending:
                for t in self.pending[semname]:
                    t.v = v
                self.pending[semname] = []
        self.lists[eng].append(entry)
        if eng == "pe" and Sched.TRACE:
            import sys as _sys
            f = _sys._getframe(1)
            names = []
            while f is not None and len(names) < 8:
                names.append(f"{f.f_code.co_name}:{f.f_lineno}")
                f = f.f_back
            self.pe_labels.append(names)
        for k in writes:
            self.last_w[k] = tok
            self.readers[k] = []
        for k in reads:
            self.readers.setdefault(k, []).append(tok)
        return tok

    def wait_all(self, eng, keys):
        toks = []
        for k in keys:
            t = self.last_w.get(k)
            if t is not None:
                toks.append(t)
            toks.extend(self.readers.get(k, ()))
        self._emit_waits(eng, toks, skip_same=False)


class Buf:
    def __init__(self, ap, atoms):
        self.ap = ap
        self.atoms = list(atoms)


def _weight_sets(phases):
    Ls, As, Bs = [], [], []
    for ph in phases:
        l_, a_, b_ = {0: ([0], [0], [0]), 1: ([1, 2], [1], [0, 1]), 2: ([3], [], [1])}[ph]
        Ls += [x for x in l_ if x not in Ls]
        As += [x for x in a_ if x not in As]
        Bs += [x for x in b_ if x not in Bs]
    return sorted(Ls), sorted(As), sorted(Bs)


def _host_consts():
    ident = np.eye(128, dtype=np.float32)
    rot = np.zeros((128, 128), np.float32)
    for a in range(2):
        for p in range(32):
            rot[a * 64 + 32 + p, a * 64 + p] = -1.0
            rot[a * 64 + p, a * 64 + 32 + p] = 1.0
    return ident, rot


def _rope_tables():
    n_rows = SEQ // 64
    pos = np.arange(SEQ)
    rows = (pos // 64).astype(np.float32)
    cols = (pos % 64).astype(np.float32)
    freqs = (np.float32(10000.0) ** (-np.arange(32, dtype=np.float32) / np.float32(32))).astype(np.float32)
    ang_r = rows[:, None] * freqs
    ang_c = cols[:, None] * freqs
    ang = np.concatenate([ang_r, ang_r, ang_c, ang_c], axis=-1).astype(np.float32)
    return np.ascontiguousarray(np.cos(ang).T.astype(np.float32)), np.ascontiguousarray(np.sin(ang).T.astype(np.float32))


def build(phases, fused):
    nc = bass.Bass("TRN2", target_bir_lowering=False)
    S = Sched()

    def din(name, shape, dt=F32):
        return nc.dram_tensor(name, list(shape), dt, kind="ExternalInput").ap()

    def dout(name, shape, dt=F32):
        return nc.dram_tensor(name, list(shape), dt, kind="ExternalOutput").ap()

    def dint(name, shape, dt=F32):
        return nc.dram_tensor(name, list(shape), dt).ap()

    first_phase, last_phase = phases[0], phases[-1]
    x_in = din("x_in", [NBLK, 128, KC * T]) if 0 in phases else None
    y_out = dout("y_out", [NBLK, 128, KC * T]) if 2 in phases else None
    mem_in = din("mem", [N_MEM, D])
    cols_in = din("cols", [128, NCOLS])
    ident_in = din("ident", [128, 128])
    rot_in = din("rot", [128, 128])
    cos_in = din("cosT", [128, TOK])
    sin_in = din("sinT", [128, TOK])
    Ls, As, Bs = _weight_sets(phases)
    Lmap = {l: i for i, l in enumerate(Ls)}
    Amap = {a: i for i, a in enumerate(As)}
    Bmap = {b: i for i, b in enumerate(Bs)}

    class _Idx:
        def __init__(self, ap, m):
            self.ap, self.m = ap, m

        def __getitem__(self, i):
            return self.ap[self.m[i]]

    w_in_a = _Idx(din("w_in_a", [max(len(As), 1), 14, 128, KC * 256]), Amap)
    w_in_b = _Idx(din("w_in_b", [max(len(Bs), 1), 12, 128, KC * 256]), Bmap)
    w_spT = din("w_spT", [2, 128, TW])
    bsp_in = din("bsp", [2, 128, TW])
    w_mem_kv = _Idx(din("w_mem_kv", [len(Ls), 4, 128, KC * 256]), Lmap)
    w_out = _Idx(din("w_out", [len(Ls), 8, 128, KC * 256]), Lmap)
    w_gu = _Idx(din("w_gate_up", [len(Ls), 44, 128, KC * 256]), Lmap)
    w_dn = _Idx(din("w_down", [len(Ls), 8, 128, FC * 256]), Lmap)

    if fused:
        xpark = dint("xpark", [NBLK, 128, KC * T])
        xpark_r = xpark_w = xpark
        kT_loc = [dint(f"kT_loc{e}", [512, TOK], BF16) for e in range(2)]
        v_loc = [dint(f"v_loc{e}", [TOK, 512], BF16) for e in range(2)]
        kT_all = [dint(f"kT_all{e}", [1024, TOK], BF16) for e in range(2)]
        v_all = [dint(f"v_all{e}", [SEQ, 512], BF16) for e in range(2)]
    else:
        xpark_r = din("xpark_in", [NBLK, 128, KC * T]) if first_phase > 0 else None
        xpark_w = dout("xpark_out", [NBLK, 128, KC * T]) if last_phase < 2 else None
        kT_loc = [None, None]
        v_loc = [None, None]
        kT_all = [None, None]
        v_all = [None, None]
        if 0 in phases:
            kT_loc[0] = dout("kT_loc0", [512, TOK], BF16)
            v_loc[0] = dout("v_loc0", [TOK, 512], BF16)
        if 1 in phases:
            kT_all[0] = din("kT_all0", [1024, TOK], BF16)
            v_all[0] = din("v_all0", [SEQ, 512], BF16)
            kT_loc[1] = dout("kT_loc1", [512, TOK], BF16)
            v_loc[1] = dout("v_loc1", [TOK, 512], BF16)
        if 2 in phases:
            kT_all[1] = din("kT_all1", [1024, TOK], BF16)
            v_all[1] = din("v_all1", [SEQ, 512], BF16)

    sb = {}
    ctxs = []

    def salloc(name, shape, dt):
        cm = nc.sbuf_tensor(name, list(shape), dt)
        t = cm.__enter__()
        ctxs.append(cm)
        sb[name] = t
        return t

    xT_t = salloc("xT", [128, KC * T], F32)
    hT_t = salloc("hT", [128, KC * T], BF16)
    NW = 5
    W_t = salloc("W", [128, NW * 4096], BF16)
    KV_t = salloc("KV", [128, 2 * 8192], BF16)
    R_t = salloc("R", [128, FC * T], BF16)
    rstd_t = salloc("rstdb", [128, T], F32)
    sq_t = salloc("sq", [128, 2 * T], BF16)
    silu_t = salloc("silu", [128, 2 * T], F32)
    recip_t = salloc("recip", [128, T], F32)
    lnt_t = salloc("lnt", [128, T], F32)
    memkv_t = salloc("memkv", [128, 2 * 2048], BF16)
    wspf_t = salloc("wspf", [128, TW], F32)
    bsp_t = salloc("bspt", [128, TW], F32)
    cols_t = salloc("colst", [128, NCOLS], F32)
    colsS_t = salloc("colsS", [128, NCOLS], F32)
    ones_t = salloc("ones", [128, 128], BF16)
    ident_t = salloc("identt", [128, 128], F32)
    rot_t = salloc("rott", [128, 128], BF16)
    small_t = salloc("small", [128, 64], F32)
    PS_cm = nc.psum_tensor("ps", [128, 8 * 512], F32)
    PS_t = PS_cm.__enter__()
    ctxs.append(PS_cm)

    xT = [Buf(xT_t[:, k * T:(k + 1) * T], [("xT", k)]) for k in range(KC)]
    hT = [Buf(hT_t[:, k * T:(k + 1) * T], [("hT", k)]) for k in range(KC)]
    Wb = [Buf(W_t[:, s * 4096:(s + 1) * 4096], [("W", s)]) for s in range(NW)]
    KTb = [Buf(KV_t[:, s * 8192:s * 8192 + 4096], [("KT", s)]) for s in range(2)]
    Vb = [Buf(KV_t[:, s * 8192 + 4096:(s + 1) * 8192], [("V", s)]) for s in range(2)]
    Ratom = lambda a0, n=1: [("R", a) for a in range(a0, a0 + n)]
    actT = [Buf(R_t[:, j * T:(j + 1) * T], Ratom(j)) for j in range(FC)]
    catT = actT[:16]
    vn = [Buf(R_t[:, (16 + 3 * tt) * T:(19 + 3 * tt) * T], Ratom(16 + 3 * tt, 3)) for tt in range(4)]
    qnb = [Buf(R_t[:, (16 + i) * T:(17 + i) * T], Ratom(16 + i)) for i in range(2)]
    PTb = [Buf(R_t[:, (18 + i) * T:(19 + i) * T], Ratom(18 + i)) for i in range(4)]
    PTb6 = PTb + [Buf(R_t[:, (42 + i) * T:(43 + i) * T], Ratom(42 + i)) for i in range(2)]

    def r32(a0):
        return Buf(R_t[:, a0 * T:(a0 + 2) * T].bitcast(F32), Ratom(a0, 2))

    cosb, sinb = r32(22), r32(24)
    tmpA, tmpB, tmpC, tmpD = r32(28), r32(30), r32(32), r32(34)
    kst = Buf(R_t[:, 36 * T:40 * T], Ratom(36, 4))
    vst = Buf(R_t[:, 40 * T:44 * T], Ratom(40, 4))
    xst = Buf(R_t[:, 0:8 * T].bitcast(F32), Ratom(0, 8))
    rstdb = Buf(rstd_t[:, :], [("rstdb", 0)])
    sqbufs = [Buf(sq_t[:, i * T:(i + 1) * T], [("sq", i)]) for i in range(2)]
    silub = [Buf(silu_t[:, i * T:(i + 1) * T], [("silu", i)]) for i in range(2)]
    recipb = Buf(recip_t[:, :], [("recip", 0)])
    lnt = Buf(lnt_t[:, :], [("lnt", 0)])
    KmT = [Buf(memkv_t[:, s * 2048:s * 2048 + 1024], [("KmT", s)]) for s in range(2)]
    Vm = [Buf(memkv_t[:, s * 2048 + 1024:(s + 1) * 2048], [("Vm", s)]) for s in range(2)]
    wspf = Buf(wspf_t[:, :], [("wspf", 0)])
    bspb = Buf(bsp_t[:, :], [("bsp", 0)])
    colsb = Buf(cols_t[:, :], [("cols", 0)])
    colsS = Buf(colsS_t[:, :], [("colsS", 0)])
    onesb = Buf(ones_t[:, :], [("ones", 0)])
    identb = Buf(ident_t[:, :], [("ident", 0)])
    rotb = Buf(rot_t[:, :], [("rot", 0)])
    smallb = Buf(small_t[:, :], [("small", 0)])
    PSb = [Buf(PS_t[:, b * 512:(b + 1) * 512], [("ps", b)]) for b in range(8)]

    st = {"ps": 0, "w": 0, "silu": 0, "qn": 0, "pt": 0, "kv": 0, "sq": 0, "pt6": 0, "acc": 0}

    reserved = set()

    def next_ps():
        while True:
            b = st["ps"]
            st["ps"] = (b + 1) % 8
            if b not in reserved:
                return PSb[b]

    def reserve_ps():
        p = next_ps()
        reserved.add(PSb.index(p))
        return p

    def release_ps(p):
        reserved.discard(PSb.index(p))

    def atoms(*bufs):
        out = []
        for b in bufs:
            out.extend(b.atoms)
        return out

    def mm(ps_ap, lhsT, rhs, start, stop, reads, writes):
        S.op("pe", lambda e: e.matmul(ps_ap, lhsT, rhs, start=start, stop=stop), reads=reads, writes=writes,
             lazy=(not stop) and LAZY_PE)

    def dma(q, out_ap, in_ap, reads, writes, sem):
        if sem not in S.sem_names:
            S.new_sem(sem)
        return S.op(q, lambda e: e.dma_start(out=out_ap, in_=in_ap), reads=reads, writes=writes, sem=sem, inc=16)

    def _small(ap, **kw):
        n = 1
        for d in ap.shape[1:]:
            n *= d
        return n < 64 or ("accum_out" in kw)

    def act(out_ap, in_ap, func, reads, writes, **kw):
        S.op("act", lambda e: e.activation(out_ap, in_ap, func, **kw), reads=reads, writes=writes,
             small=_small(out_ap, **kw))

    def dve(fn, reads, writes):
        S.op("dve", fn, reads=reads, writes=writes)

    def dve_tt(out, in0, in1, op, reads, writes):
        S.op("dve", lambda e: e.tensor_tensor(out, in0, in1, op), reads=reads, writes=writes, small=_small(out))

    def dve_ts(out, in0, s1, s2, op0, op1, reads, writes):
        if op1 is None:
            S.op("dve", lambda e: e.tensor_scalar(out, in0, s1, None, op0), reads=reads, writes=writes, small=_small(out))
        else:
            S.op("dve", lambda e: e.tensor_scalar(out, in0, s1, s2, op0, op1), reads=reads, writes=writes, small=_small(out))

    def dve_stt(out, in0, sc, in1, op0, op1, reads, writes):
        S.op("dve", lambda e: e.scalar_tensor_tensor(out, in0, sc, in1, op0, op1), reads=reads, writes=writes, small=_small(out))

    def pool_tt(out, in0, in1, op, reads, writes):
        S.op("pool", lambda e: e.tensor_tensor(out, in0, in1, op), reads=reads, writes=writes)

    def pool_copy(out, in_, reads, writes):
        S.op("pool", lambda e: e.tensor_copy(out, in_), reads=reads, writes=writes)

    def dve_copy(out, in_, reads, writes):
        S.op("dve", lambda e: e.tensor_copy(out, in_), reads=reads, writes=writes, small=_small(out))

    def dve_recip(out, in_, reads, writes):
        S.op("dve", lambda e: e.reciprocal(out, in_), reads=reads, writes=writes, small=_small(out))

    def dve_rsum(out, in_, reads, writes):
        S.op("dve", lambda e: e.reduce_sum(out, in_, AX.X), reads=reads, writes=writes, small=True)

    def pe_tr(out, in_, reads, writes):
        S.op("pe", lambda e: e.transpose(out, in_, identb.ap), reads=reads + identb.atoms, writes=writes)

    def wload(w2d, r0, nk, c0, ncols):
        s = st["w"]
        st["w"] = (s + 1) % NW
        wgen[s] = wgen.get(s, 0) + 1
        assert ncols == 256 and c0 % 256 == 0
        flat = Wb[s].ap[:, 0:nk * ncols]
        src = w2d[c0 // 256][:, r0 * 256:(r0 + nk) * 256]
        dma("pool", flat, src, reads=[], writes=Wb[s].atoms, sem=f"w{s}")
        return flat.rearrange("p (k c) -> p k c", k=nk), Wb[s]

    wgen = {}

    gcol = lambda c: colsb.ap[:, c:c + 1]
    gcolS = lambda c: colsS.ap[:, c:c + 1]

    dma("sp", colsb.ap, cols_in, [], colsb.atoms, "c0")
    dma("sp", identb.ap, ident_in, [], identb.atoms, "c1")
    dma("pool", rotb.ap, rot_in, [], rotb.atoms, "c2")
    S.op("dve", lambda e: e.memset(onesb.ap, 1.0), writes=onesb.atoms)
    ones32_t = salloc("ones32", [128, 128], F32)
    ones32 = Buf(ones32_t[:, :], [("ones32", 0)])
    S.op("dve", lambda e: e.memset(ones32.ap, 1.0), writes=ones32.atoms)
    epst = salloc("epst", [128, 4], F32)
    epsb = {}
    for i, n in enumerate((2048, 128, 1536)):
        S.op("dve", (lambda i=i, n=n: (lambda e: e.memset(epst[:, i:i + 1], float(n * EPS))))(), writes=[("eps", 0)], small=True)
        epsb[n] = epst[:, i:i + 1]
    for (c0, c1, n) in ((C_GMIX, C_GMQ, 2048.0), (C_GMQ, C_GV, 128.0), (C_GV, NCOLS, 1536.0)):
        dve_ts(colsS.ap[:, c0:c1], colsb.ap[:, c0:c1], float(np.sqrt(n)), None, ALU.mult, None,
               colsb.atoms, colsS.atoms)

    def rstd_from(out_ap, ss_ap, n, reads, writes):
        w = out_ap.shape[-1]
        act(lnt.ap[:, 0:w], ss_ap, AF.Ln, reads=list(reads) + [("eps", 0)], writes=lnt.atoms, bias=epsb[n], scale=1.0)
        act(out_ap, lnt.ap[:, 0:w], AF.Exp, reads=lnt.atoms, writes=writes, scale=-0.5)

    def next_sq():
        b = sqbufs[st["sq"]]
        st["sq"] ^= 1
        return b

    def mem_kv(l, slot):
        hm = hT
        for mt in range(2):
            dma("sp", xst.ap, mem_in[mt * 128:(mt + 1) * 128, :], [], xst.atoms, "xst")
            for q in range(4):
                act(tmpA.ap, xst.ap[:, q * 512:(q + 1) * 512], AF.Square,
                    reads=xst.atoms, writes=tmpA.atoms + smallb.atoms, accum_out=smallb.ap[:, q:q + 1])
            dve_rsum(smallb.ap[:, 4:5], smallb.ap[:, 0:4], smallb.atoms, smallb.atoms)
            rstd_from(smallb.ap[:, 5:6], smallb.ap[:, 4:5], 2048, smallb.atoms, smallb.atoms)
            dve_ts(xst.ap, xst.ap, smallb.ap[:, 5:6], None, ALU.mult, None, xst.atoms + smallb.atoms, xst.atoms)
            for k in range(KC):
                ps = next_ps()
                pe_tr(ps.ap[:, 0:128], xst.ap[:, k * 128:(k + 1) * 128], xst.atoms, ps.atoms)
                act(hm[k].ap[:, mt * 128:(mt + 1) * 128], ps.ap[:, 0:128], AF.Copy,
                    reads=ps.atoms + colsS.atoms, writes=hm[k].atoms, scale=gcolS(C_GMEM + l * 16 + k))
        wl = w_mem_kv[l]
        for half in range(2):
            wv, wb = wload(wl, 0, KC, half * 256, 256)
            for hh in range(2):
                h = half * 2 + hh
                ps = next_ps()
                for k in range(KC):
                    mm(ps.ap[:, 0:256], wv[:, k, hh * 128:(hh + 1) * 128], hm[k].ap[:, 0:256], k == 0, k == KC - 1,
                       reads=wb.atoms + hm[k].atoms, writes=ps.atoms)
                head_norm(ps, 256, C_GMK + l, KmT[slot].ap[:, h * 256:(h + 1) * 256], KmT[slot].atoms)
        pss = [next_ps(), next_ps()]
        for half in range(2):
            wv, wb = wload(wl, 0, KC, 512 + half * 256, 256)
            for mt in range(2):
                for k in range(KC):
                    mm(pss[mt].ap[:, half * 256:(half + 1) * 256], hm[k].ap[:, mt * 128:(mt + 1) * 128], wv[:, k, :],
                       k == 0, k == KC - 1, reads=wb.atoms + hm[k].atoms, writes=pss[mt].atoms)
        for mt in range(2):
            act(Vm[slot].ap[:, mt * 512:(mt + 1) * 512], pss[mt].ap, AF.Copy, reads=pss[mt].atoms, writes=Vm[slot].atoms)

    def norm_block(gbase):
        ps = next_ps()
        for k in range(KC):
            sq = next_sq()
            act(sq.ap, xT[k].ap, AF.Square, reads=xT[k].atoms, writes=sq.atoms)
            mm(ps.ap, onesb.ap, sq.ap, k == 0, k == KC - 1, reads=onesb.atoms + sq.atoms, writes=ps.atoms)
        rstd_from(rstdb.ap, ps.ap, 2048, ps.atoms, rstdb.atoms)
        for k in range(KC):
            dve_stt(hT[k].ap, xT[k].ap, gcolS(gbase + k), rstdb.ap, ALU.mult, ALU.mult,
                    xT[k].atoms + rstdb.atoms + colsS.atoms, hT[k].atoms)

    def head_norm(ps, n, gidx, out_ap, out_atoms):
        sq = next_sq()
        act(sq.ap[:, 0:n], ps.ap[:, 0:n], AF.Square, reads=ps.atoms, writes=sq.atoms)
        ps2 = next_ps()
        mm(ps2.ap[:, 0:n], onesb.ap, sq.ap[:, 0:n], True, True, reads=onesb.atoms + sq.atoms, writes=ps2.atoms)
        rstd_from(rstdb.ap[:, 0:n], ps2.ap[:, 0:n], 128, ps2.atoms, rstdb.atoms)
        dve_stt(out_ap, ps.ap[:, 0:n], gcolS(gidx), rstdb.ap[:, 0:n], ALU.mult, ALU.mult,
                ps.atoms + rstdb.atoms + colsS.atoms, out_atoms)

    def rope(qb, out_ap, out_atoms):
        ps = next_ps()
        mm(ps.ap, rotb.ap, qb.ap, True, True, reads=rotb.atoms + qb.atoms, writes=ps.atoms)
        dve_tt(tmpA.ap, qb.ap, cosb.ap, ALU.mult, qb.atoms + cosb.atoms, tmpA.atoms)
        dve_tt(tmpB.ap, ps.ap, sinb.ap, ALU.mult, ps.atoms + sinb.atoms, tmpB.atoms)
        dve_tt(out_ap, tmpA.ap, tmpB.ap, ALU.add, tmpA.atoms + tmpB.atoms, out_atoms)

    def proj_norm4(wl, c0, gidx, rst, sqs, outs, rope_to=None):
        psq = []
        for half in range(2):
            wv, wb = wload(wl, 0, KC, c0 + half * 256, 256)
            for hh in range(2):
                ps = next_ps()
                psq.append(ps)
                for k in range(KC):
                    mm(ps.ap, wv[:, k, hh * 128:(hh + 1) * 128], hT[k].ap, k == 0, k == KC - 1,
                       reads=wb.atoms + hT[k].atoms, writes=ps.atoms)
        for h in range(4):
            act(sqs[h].ap, psq[h].ap, AF.Square, reads=psq[h].atoms, writes=sqs[h].atoms)
        ps2 = []
        for h in range(4):
            p = next_ps()
            ps2.append(p)
            mm(p.ap, onesb.ap, sqs[h].ap, True, True, reads=onesb.atoms + sqs[h].atoms, writes=p.atoms)
        for h in range(4):
            rstd_from(rst[h].ap, ps2[h].ap, 128, ps2[h].atoms, rst[h].atoms)
        for h in range(4):
            dve_stt(outs[h].ap, psq[h].ap, gcolS(gidx), rst[h].ap, ALU.mult, ALU.mult,
                    psq[h].atoms + rst[h].atoms + colsS.atoms, outs[h].atoms)
        if rope_to is not None:
            for h in range(4):
                mm(ps2[h].ap, rotb.ap, outs[h].ap, True, True, reads=rotb.atoms + outs[h].atoms, writes=ps2[h].atoms)
            for h in range(4):
                o_ap, o_atoms = rope_to[h]
                dve_tt(tmpC.ap, outs[h].ap, cosb.ap, ALU.mult, outs[h].atoms + cosb.atoms, tmpC.atoms)
                dve_tt(tmpD.ap, ps2[h].ap, sinb.ap, ALU.mult, ps2[h].atoms + sinb.atoms, tmpD.atoms)
                dve_tt(o_ap, tmpC.ap, tmpD.ap, ALU.add, tmpC.atoms + tmpD.atoms, o_atoms)

    def mem_core(slot, qbufs):
        def mem_S(h):
            pts = []
            for c in range(2):
                pss = next_ps()
                mm(pss.ap, KmT[slot].ap[:, h * 256 + c * 128:h * 256 + (c + 1) * 128], qbufs[h].ap, True, True,
                   reads=KmT[slot].atoms + qbufs[h].atoms, writes=pss.atoms)
                pt = next_pt()
                act(pt.ap, pss.ap, AF.Exp, reads=pss.atoms, writes=pt.atoms, scale=float(SCALE))
                pts.append(pt)
            return pts

        nxt = mem_S(0)
        for h in range(4):
            pts = nxt
            pso2, psm2 = next_ps(), next_ps()
            if h + 1 < 4:
                reserved.update({PSb.index(pso2), PSb.index(psm2)})
                nxt = mem_S(h + 1)
                reserved.difference_update({PSb.index(pso2), PSb.index(psm2)})
            for c in range(2):
                mm(pso2.ap, Vm[slot].ap[:, c * 512 + h * 128:c * 512 + (h + 1) * 128], pts[c].ap, c == 0, c == 1,
                   reads=Vm[slot].atoms + pts[c].atoms, writes=pso2.atoms)
            for c in range(2):
                mm(psm2.ap, onesb.ap, pts[c].ap, c == 0, c == 1, reads=onesb.atoms + pts[c].atoms, writes=psm2.atoms)
            act(lnt.ap, psm2.ap, AF.Ln, reads=psm2.atoms, writes=lnt.atoms)
            act(recipb.ap, lnt.ap, AF.Exp, reads=lnt.atoms, writes=recipb.atoms, scale=-1.0)
            dve_tt(catT[12 + h].ap, pso2.ap, recipb.ap, ALU.mult, pso2.atoms + recipb.atoms, catT[12 + h].atoms)

    def next_qn():
        b = qnb[st["qn"]]
        st["qn"] ^= 1
        return b

    def next_pt():
        b = PTb[st["pt"] % 4]
        st["pt"] = (st["pt"] + 1) % 4
        return b

    def next_pt6():
        b = PTb6[st["pt6"]]
        st["pt6"] = (st["pt6"] + 1) % 6
        return b

    def mem_attention(l, slot, wl, qc0):
        for half in range(2):
            wv, wb = wload(wl, 0, KC, qc0 + half * 256, 256)
            for hh in range(2):
                h = half * 2 + hh
                ps = next_ps()
                for k in range(KC):
                    mm(ps.ap, wv[:, k, hh * 128:(hh + 1) * 128], hT[k].ap, k == 0, k == KC - 1,
                       reads=wb.atoms + hT[k].atoms, writes=ps.atoms)
                qb = next_qn()
                head_norm(ps, T, C_GMQ + l, qb.ap, qb.atoms)
                pts = []
                for c in range(2):
                    pss = next_ps()
                    mm(pss.ap, KmT[slot].ap[:, h * 256 + c * 128:h * 256 + (c + 1) * 128], qb.ap, True, True,
                       reads=KmT[slot].atoms + qb.atoms, writes=pss.atoms)
                    pt = next_pt()
                    act(pt.ap, pss.ap, AF.Exp, reads=pss.atoms, writes=pt.atoms, scale=float(SCALE))
                    pts.append(pt)
                pso, psm = next_ps(), next_ps()
                for c in range(2):
                    mm(pso.ap, Vm[slot].ap[:, c * 512 + h * 128:c * 512 + (h + 1) * 128], pts[c].ap, c == 0, c == 1,
                       reads=Vm[slot].atoms + pts[c].atoms, writes=pso.atoms)
                for c in range(2):
                    mm(psm.ap, onesb.ap, pts[c].ap, c == 0, c == 1, reads=onesb.atoms + pts[c].atoms, writes=psm.atoms)
                dve_recip(recipb.ap, psm.ap, psm.atoms, recipb.atoms)
                dve_tt(catT[12 + h].ap, pso.ap, recipb.ap, ALU.mult, pso.atoms + recipb.atoms, catT[12 + h].atoms)

    def out_proj(l):
        wl = w_out[l]
        for mp in range(8):
            wv, wb = wload(wl, 0, KC, mp * 256, 256)
            for mi in range(2):
                m = mp * 2 + mi
                ps = next_ps()
                for c in range(KC):
                    mm(ps.ap, wv[:, c, mi * 128:(mi + 1) * 128], catT[c].ap, c == 0, c == KC - 1,
                       reads=wb.atoms + catT[c].atoms, writes=ps.atoms)
                dve_tt(xT[m].ap, xT[m].ap, ps.ap, ALU.add, xT[m].atoms + ps.atoms, xT[m].atoms)

    def ffn(l):
        norm_block(C_GFFN + l * 16)
        wl = w_gu[l]
        for jp in range(FC // 2):
            wg, wgb = wload(wl, 0, KC, jp * 256, 256)
            wu, wub = wload(wl, 0, KC, DFF + jp * 256, 256)
            for ji in range(2):
                j = jp * 2 + ji
                psg, psu = next_ps(), next_ps()
                for k in range(KC):
                    mm(psg.ap, wg[:, k, ji * 128:(ji + 1) * 128], hT[k].ap, k == 0, k == KC - 1,
                       reads=wgb.atoms + hT[k].atoms, writes=psg.atoms)
                for k in range(KC):
                    mm(psu.ap, wu[:, k, ji * 128:(ji + 1) * 128], hT[k].ap, k == 0, k == KC - 1,
                       reads=wub.atoms + hT[k].atoms, writes=psu.atoms)
                sb_ = silub[st["silu"]]
                st["silu"] ^= 1
                act(sb_.ap, psg.ap, AF.Silu, reads=psg.atoms, writes=sb_.atoms)
                dve_tt(actT[j].ap, sb_.ap, psu.ap, ALU.mult, sb_.atoms + psu.atoms, actT[j].atoms)
        wl = w_dn[l]
        fsplit = [(0, 16), (16, 16), (32, 12)]
        for mp in range(8):
            pss = [next_ps(), next_ps()]
            for (f0, nf) in fsplit:
                wv, wb = wload(wl, f0, nf, mp * 256, 256)
                for mi in range(2):
                    for f in range(nf):
                        fg = f0 + f
                        mm(pss[mi].ap, wv[:, f, mi * 128:(mi + 1) * 128], actT[fg].ap, fg == 0, fg == FC - 1,
                           reads=wb.atoms + actT[fg].atoms, writes=pss[mi].atoms)
            for mi in range(2):
                m = mp * 2 + mi
                dve_tt(xT[m].ap, xT[m].ap, pss[mi].ap, ALU.add, xT[m].atoms + pss[mi].atoms, xT[m].atoms)

    wsc_all = [Buf(R_t[:, (32 + 3 * tt) * T:(35 + 3 * tt) * T], Ratom(32 + 3 * tt, 3)) for tt in range(4)]

    def layer_a_mixer(l, slot):
        ia = l // 2
        wl = w_in_a[ia]
        norm_block(C_GMIX + l * 16)
        dma("sp", wspf.ap, w_spT[ia], [], wspf.atoms, "wspf")
        dma("sp", bspb.ap, bsp_in[ia], [], bspb.atoms, "bsp")
        for wt in range(6):
            wv, wb = wload(wl, 0, KC, TW + wt * 256, 256)
            for tt in range(4):
                ps = next_ps()
                for k in range(KC):
                    mm(ps.ap[:, 0:256], hT[k].ap[:, tt * 128:(tt + 1) * 128], wv[:, k, :], k == 0, k == KC - 1,
                       reads=wb.atoms + hT[k].atoms, writes=ps.atoms)
                act(tmpA.ap[:, 0:256], ps.ap[:, 0:256], AF.Gelu, reads=ps.atoms, writes=tmpA.atoms)
                act(tmpB.ap[:, 0:256], tmpA.ap[:, 0:256], AF.Square, reads=tmpA.atoms, writes=tmpB.atoms + smallb.atoms,
                    accum_out=smallb.ap[:, 8 + tt * 6 + wt:9 + tt * 6 + wt])
                dve_copy(vn[tt].ap[:, wt * 256:(wt + 1) * 256], tmpA.ap[:, 0:256], tmpA.atoms, vn[tt].atoms)
        dve_rsum(smallb.ap[:, 32:36], smallb.ap[:, 8:32].rearrange("p (a b) -> p a b", a=4), smallb.atoms, smallb.atoms)
        rstd_from(smallb.ap[:, 36:40], smallb.ap[:, 32:36], 1536, smallb.atoms, smallb.atoms)
        for tt in range(4):
            dve_ts(wsc_all[tt].ap, wspf.ap, smallb.ap[:, 36 + tt:37 + tt], None, ALU.mult, None,
                   wspf.atoms + smallb.atoms, wsc_all[tt].atoms)
        for up in range(6):
            wv, wb = wload(wl, 0, KC, up * 256, 256)
            for gi in range(2):
                g = up * 2 + gi
                psu = next_ps()
                for k in range(KC):
                    mm(psu.ap, wv[:, k, gi * 128:(gi + 1) * 128], hT[k].ap, k == 0, k == KC - 1,
                       reads=wb.atoms + hT[k].atoms, writes=psu.atoms)
                act(tmpA.ap, psu.ap, AF.Gelu, reads=psu.atoms, writes=tmpA.atoms)
                pss = next_ps()
                for tt in range(4):
                    mm(pss.ap[:, tt * 128:(tt + 1) * 128], vn[tt].ap[:, g * 128:(g + 1) * 128],
                       wsc_all[tt].ap[:, g * 128:(g + 1) * 128], True, True,
                       reads=vn[tt].atoms + wsc_all[tt].atoms, writes=pss.atoms)
                dve_stt(tmpB.ap.rearrange("p (a b) -> p a b", a=4), pss.ap.rearrange("p (a b) -> p a b", a=4),
                        gcolS(C_GV + ia * 12 + g),
                        bspb.ap[:, g * 128:(g + 1) * 128].unsqueeze(1).broadcast_to([128, 4, 128]),
                        ALU.mult, ALU.add, pss.atoms + colsS.atoms + bspb.atoms, tmpB.atoms)
                dve_tt(catT[g].ap, tmpA.ap, tmpB.ap, ALU.mult, tmpA.atoms + tmpB.atoms, catT[g].atoms)
        qbufs = [Buf(R_t[:, (22 + h) * T:(23 + h) * T], Ratom(22 + h)) for h in range(4)]
        proj_norm4(wl, 2 * TW, C_GMQ + l, [r32(32), r32(34), r32(36), r32(38)], PTb, qbufs)
        mem_core(slot, qbufs)

    def load_rope_tables(blk):
        dma("sp", cosb.ap, cos_in[:, blk * T:(blk + 1) * T], [], cosb.atoms, "cos")
        dma("sp", sinb.ap, sin_in[:, blk * T:(blk + 1) * T], [], sinb.atoms, "sin")

    def layer_b_kv(l, blk, e):
        ib = l // 2
        wl = w_in_b[ib]
        norm_block(C_GMIX + l * 16)
        load_rope_tables(blk)
        kq = [Buf(R_t[:, (8 + h) * T:(9 + h) * T], Ratom(8 + h)) for h in range(4)]
        proj_norm4(wl, TW, C_GK + ib, [r32(0), r32(2), r32(4), r32(6)], PTb, kq,
                   rope_to=[(kst.ap[:, h * T:(h + 1) * T], kst.atoms) for h in range(4)])
        dma("sp", kT_loc[e].rearrange("(h d) t -> d h t", d=128)[:, :, blk * T:(blk + 1) * T],
            kst.ap.rearrange("p (h t) -> p h t", h=4), kst.atoms, [("kTloc", e, blk)], "kst")
        pss = [next_ps() for _ in range(4)]
        for half in range(2):
            wv, wb = wload(wl, 0, KC, TW + 512 + half * 256, 256)
            for tt in range(4):
                for k in range(KC):
                    mm(pss[tt].ap[:, half * 256:(half + 1) * 256], hT[k].ap[:, tt * 128:(tt + 1) * 128], wv[:, k, :],
                       k == 0, k == KC - 1, reads=wb.atoms + hT[k].atoms, writes=pss[tt].atoms)
        for tt in range(4):
            act(vst.ap[:, tt * T:(tt + 1) * T], pss[tt].ap, AF.Copy, reads=pss[tt].atoms, writes=vst.atoms)
        dma("sp", v_loc[e][blk * T:(blk + 1) * T, :].rearrange("(tt p) c -> p tt c", p=128),
            vst.ap.rearrange("p (tt c) -> p tt c", tt=4), vst.atoms, [("vloc", e, blk)], "vst")

    qrb = [Buf(tmpC.ap.bitcast(BF16)[:, 0:T], tmpC.atoms), Buf(tmpD.ap.bitcast(BF16)[:, 0:T], tmpD.atoms)]

    qmnb = [Buf(R_t[:, (36 + i) * T:(37 + i) * T], Ratom(36 + i)) for i in range(4)]
    accb = [r32(26), r32(40)]

    def layer_b_attn(l, blk, e, slot):
        ib = l // 2
        wl = w_in_b[ib]
        norm_block(C_GMIX + l * 16)
        load_rope_tables(blk)
        prepA, prepB = reserve_ps(), reserve_ps()
        wq = {}

        def qtile(c0):
            cb = (c0 // 256) * 256
            if cb in wq:
                wv, wb, g = wq[cb]
                if wgen[Wb.index(wb)] != g:
                    del wq[cb]
            if cb not in wq:
                wv, wb = wload(wl, 0, KC, cb, 256)
                wq[cb] = (wv, wb, wgen[Wb.index(wb)])
            wv, wb, _ = wq[cb]
            return wv, wb, c0 - cb

        def prep_gen(c0, gidx, out_buf, do_rope):
            wv, wb, off = qtile(c0)
            for k in range(KC):
                mm(prepA.ap, wv[:, k, off:off + 128], hT[k].ap, k == 0, k == KC - 1,
                   reads=wb.atoms + hT[k].atoms, writes=prepA.atoms)
                if k % 2 == 1:
                    yield
            sq = next_sq()
            act(sq.ap, prepA.ap, AF.Square, reads=prepA.atoms, writes=sq.atoms)
            mm(prepB.ap, onesb.ap, sq.ap, True, True, reads=onesb.atoms + sq.atoms, writes=prepB.atoms)
            yield
            rstd_from(rstdb.ap, prepB.ap, 128, prepB.atoms, rstdb.atoms)
            if do_rope:
                qb = next_qn()
                dve_stt(qb.ap, prepA.ap, gcolS(gidx), rstdb.ap, ALU.mult, ALU.mult,
                        prepA.atoms + rstdb.atoms + colsS.atoms, qb.atoms)
                mm(prepB.ap, rotb.ap, qb.ap, True, True, reads=rotb.atoms + qb.atoms, writes=prepB.atoms)
                yield
                dve_tt(tmpA.ap, qb.ap, cosb.ap, ALU.mult, qb.atoms + cosb.atoms, tmpA.atoms)
                dve_tt(tmpB.ap, prepB.ap, sinb.ap, ALU.mult, prepB.atoms + sinb.atoms, tmpB.atoms)
                dve_tt(out_buf.ap, tmpA.ap, tmpB.ap, ALU.add, tmpA.atoms + tmpB.atoms, out_buf.atoms)
            else:
                dve_stt(out_buf.ap, prepA.ap, gcolS(gidx), rstdb.ap, ALU.mult, ALU.mult,
                        prepA.atoms + rstdb.atoms + colsS.atoms, out_buf.atoms)
            yield

        def run_all(g):
            for _ in g:
                pass

        def q_gen(qh):
            return prep_gen(qh * 128, C_GQ + ib, qrb[qh % 2], True)

        def m_gen(h):
            return prep_gen(TW + 1024 + h * 128, C_GMQ + l, qmnb[h], False)

        qtile(0)
        run_all(q_gen(0))
        psos, psm = [reserve_ps(), reserve_ps()], reserve_ps()
        LOOK = 2
        for kvh in range(4):
            s = st["kv"]
            st["kv"] ^= 1
            dma("sp", KTb[s].ap.rearrange("p (r t) -> p r t", r=2),
                kT_all[e].rearrange("(r h d) t -> h d r t", r=2, h=4)[kvh],
                [("kTall", e)], KTb[s].atoms, f"kt{s}")
            for jq in range(4):
                dma("sp", Vb[s].ap.rearrange("p (j c) -> p j c", j=32)[:, jq * 8:(jq + 1) * 8, :],
                    v_all[e].rearrange("(j p) c -> p j c", p=128)[:, jq * 8:(jq + 1) * 8, kvh * 128:(kvh + 1) * 128],
                    [("vall", e)], Vb[s].atoms, f"vv{s}")
            for qi in range(3):
                qh = kvh * 3 + qi
                qr = qrb[qh % 2]
                pso = psos[qh % 2]
                if qh + 2 < 12:
                    qtile((qh + 2) * 128)
                if 3 <= qh < 7:
                    qtile(TW + 1024 + (qh - 3) * 128)
                gens = []
                sched = {}
                if qh + 1 < 12:
                    g = q_gen(qh + 1)
                    gens.append(g)
                    for jj in list(range(1, 9)) + [10, 14, 18]:
                        sched[jj] = g
                if 4 <= qh < 8:
                    g = m_gen(qh - 4)
                    gens.append(g)
                    for jj in list(range(19, 27)) + [28, 30]:
                        sched[jj] = g

                acc = accb[st["acc"]]
                accp = silub[st["acc"]]
                st["acc"] ^= 1

                def emit_S(j):
                    pss = next_ps()
                    mm(pss.ap, KTb[s].ap[:, j * 128:(j + 1) * 128], qr.ap, True, True,
                       reads=KTb[s].atoms + qr.atoms, writes=pss.atoms)
                    pt = next_pt6()
                    act(pt.ap, pss.ap, AF.Exp, reads=pss.atoms, writes=pt.atoms, scale=float(SCALE))
                    a_ = acc if j % 2 == 0 else accp
                    if j < 2:
                        dve_copy(a_.ap, pt.ap, pt.atoms, a_.atoms)
                    else:
                        dve_tt(a_.ap, a_.ap, pt.ap, ALU.add, a_.atoms + pt.atoms, a_.atoms)
                    return pt

                pts = {}
                for j in range(LOOK):
                    pts[j] = emit_S(j)
                for j in range(32):
                    if j + LOOK < 32:
                        pts[j + LOOK] = emit_S(j + LOOK)
                    pt = pts.pop(j)
                    mm(pso.ap, Vb[s].ap[:, j * 128:(j + 1) * 128], pt.ap, j == 0, j == 31,
                       reads=Vb[s].atoms + pt.atoms, writes=pso.atoms)
                    if j in sched:
                        next(sched[j], None)
                for g in gens:
                    run_all(g)
                mm(psm.ap, ones32.ap, acc.ap, True, False, reads=ones32.atoms + acc.atoms, writes=psm.atoms)
                mm(psm.ap, ones32.ap, accp.ap, False, True, reads=ones32.atoms + accp.atoms, writes=psm.atoms)
                act(lnt.ap, psm.ap, AF.Ln, reads=psm.atoms, writes=lnt.atoms)
                act(recipb.ap, lnt.ap, AF.Exp, reads=lnt.atoms, writes=recipb.atoms, scale=-1.0)
                dve_tt(catT[qh].ap, pso.ap, recipb.ap, ALU.mult, pso.atoms + recipb.atoms, catT[qh].atoms)
        release_ps(prepA)
        release_ps(prepB)
        release_ps(psos[0])
        release_ps(psos[1])
        release_ps(psm)
        mem_core(slot, qmnb)

    def load_x_tokenmajor(blk):
        for k in range(KC):
            dma("sp", xT[k].ap, x_in[blk][:, k * T:(k + 1) * T], [], xT[k].atoms, f"xl{k}")

    def store_x_tokenmajor(blk):
        for k in range(KC):
            dma("sp", y_out[blk][:, k * T:(k + 1) * T], xT[k].ap, xT[k].atoms, [("yout", blk, k)], f"xs{k}")

    def load_x_park(blk):
        for k in range(KC):
            dma("sp", xT[k].ap, xpark_r[blk][:, k * T:(k + 1) * T], [("xpark", blk, k)], xT[k].atoms, f"xl{k}")

    def store_x_park(blk):
        for k in range(KC):
            dma("sp", xpark_w[blk][:, k * T:(k + 1) * T], xT[k].ap, xT[k].atoms, [("xpark", blk, k)], f"xs{k}")

    def exchange(e):
        groups = [[0, 1], [2, 3], [4, 5], [6, 7]]
        rk = [("kTloc", e, b) for b in range(NBLK)]
        rv = [("vloc", e, b) for b in range(NBLK)]
        S.new_sem(f"cck{e}")
        S.new_sem(f"ccv{e}")
        S.op("pool", lambda en: en.collective_compute("AllGather", ALU.bypass, replica_groups=groups,
                                                      ins=[kT_loc[e]], outs=[kT_all[e]]),
             reads=rk, writes=[("kTall", e)], sem=f"cck{e}", inc=CC_INC)
        S.op("pool", lambda en: en.collective_compute("AllGather", ALU.bypass, replica_groups=groups,
                                                      ins=[v_loc[e]], outs=[v_all[e]]),
             reads=rv, writes=[("vall", e)], sem=f"ccv{e}", inc=CC_INC)

    def a_layer(l, slot):
        layer_a_mixer(l, slot)
        out_proj(l)
        ffn(l)

    for ph in phases:
        mem_layers = {0: [0], 1: [1, 2], 2: [3]}[ph]
        slots = {}
        for i, l in enumerate(mem_layers):
            mem_kv(l, i)
            slots[l] = i
        for blk in range(NBLK):
            if ph == 0:
                load_x_tokenmajor(blk)
                a_layer(0, slots[0])
                layer_b_kv(1, blk, 0)
                store_x_park(blk)
            elif ph == 1:
                load_x_park(blk)
                layer_b_attn(1, blk, 0, slots[1])
                out_proj(1)
                ffn(1)
                a_layer(2, slots[2])
                layer_b_kv(3, blk, 1)
                store_x_park(blk)
            else:
                load_x_park(blk)
                layer_b_attn(3, blk, 1, slots[3])
                out_proj(3)
                ffn(3)
                store_x_tokenmajor(blk)
        if fused and ph < 2:
            exchange(ph)

    out_keys = []
    if 2 in phases:
        out_keys += [("yout", b, k) for b in range(NBLK) for k in range(KC)]
    if not fused:
        if last_phase < 2:
            out_keys += [("xpark", b, k) for b in range(NBLK) for k in range(KC)]
            e = last_phase
            out_keys += [("kTloc", e, b) for b in range(NBLK)] + [("vloc", e, b) for b in range(NBLK)]
    S.wait_all("sp", out_keys)

    sem_cms = {name: nc.semaphore(name) for name in S.sem_names}
    sems = {}
    for name, cm in sem_cms.items():
        sems[name] = cm.__enter__()
        ctxs.append(cm)

    def replay(e, items):
        for it in items:
            if it[0] == "wait":
                e.wait_ge(sems[it[1]], it[2])
            else:
                _, fn, semname, inc = it
                ins = fn(e)
                if inc:
                    ins.then_inc(sems[semname], inc)

    with nc.Block() as block:
        @block.tensor
        def _(e):
            replay(e, S.lists["pe"])

        @block.scalar
        def _(e):
            replay(e, S.lists["act"])

        @block.vector
        def _(e):
            replay(e, S.lists["dve"])

        @block.gpsimd
        def _(e):
            replay(e, S.lists["pool"])

        @block.sync
        def _(e):
            replay(e, S.lists["sp"])

    for cm in reversed(ctxs):
        cm.__exit__(None, None, None)
    return nc, ({k: len(v) for k, v in S.lists.items()} if not Sched.TRACE else S.pe_labels)


CC_INC = 1
LAZY_PE = True


def _slice_weights(com, phases):
    Ls, As, Bs = _weight_sets(phases)
    d = dict(com)
    d["w_in_a"] = np.ascontiguousarray(com["w_in_a"][As]) if As else np.ascontiguousarray(com["w_in_a"][:1])
    d["w_in_b"] = np.ascontiguousarray(com["w_in_b"][Bs]) if Bs else np.ascontiguousarray(com["w_in_b"][:1])
    for k in ("w_mem_kv", "w_out", "w_gate_up", "w_down"):
        d[k] = np.ascontiguousarray(com[k][Ls])
    return d


def _common_inputs(inputs):
    f = lambda a: np.ascontiguousarray(np.asarray(a, dtype=np.float32))
    cols = np.zeros((128, NCOLS), np.float32)

    def colize(v):
        return np.asarray(v, np.float32).reshape(-1, 128).T

    for l in range(4):
        cols[:, C_GMIX + l * 16:C_GMIX + (l + 1) * 16] = colize(inputs["g_mix"][l])
        cols[:, C_GFFN + l * 16:C_GFFN + (l + 1) * 16] = colize(inputs["g_ffn"][l])
        cols[:, C_GMEM + l * 16:C_GMEM + (l + 1) * 16] = colize(inputs["g_mem"][l])
        cols[:, C_GMQ + l] = np.asarray(inputs["g_mq"][l], np.float32)
        cols[:, C_GMK + l] = np.asarray(inputs["g_mk"][l], np.float32)
    for i in range(2):
        cols[:, C_GQ + i] = np.asarray(inputs["g_q_b"][i], np.float32)
        cols[:, C_GK + i] = np.asarray(inputs["g_k_b"][i], np.float32)
        cols[:, C_GV + i * 12:C_GV + (i + 1) * 12] = colize(inputs["g_v_a"][i])
    ident, rot = _host_consts()
    wsp = np.asarray(inputs["w_spatial"], np.float32)
    w_spT = np.ascontiguousarray(wsp.transpose(0, 3, 1, 2).reshape(2, 128, TW))
    bsp = np.ascontiguousarray(np.broadcast_to(np.asarray(inputs["b_spatial"], np.float32).reshape(2, 1, TW), (2, 128, TW)))
    def tile(w):
        w = np.asarray(w, np.float32)
        L, R, C = w.shape
        nk = R // 128
        return np.ascontiguousarray(w.reshape(L, nk, 128, C // 256, 256).transpose(0, 3, 2, 1, 4)).reshape(L, C // 256, 128, nk * 256)

    com = {
        "cols": cols, "ident": ident, "rot": rot, "w_spT": w_spT, "bsp": bsp,
        "w_in_a": tile(inputs["w_in_a"]), "w_in_b": tile(inputs["w_in_b"]), "w_mem_kv": tile(inputs["w_mem_kv"]),
        "w_out": tile(inputs["w_out"]), "w_gate_up": tile(inputs["w_gate_up"]), "w_down": tile(inputs["w_down"]),
    }
    return com


_CACHE = {}


def _get_prog(phases, fused):
    key = (tuple(phases), fused)
    if key not in _CACHE:
        _CACHE[key] = build(list(phases), fused)[0]
    return _CACHE[key]


def _to_blocks(xs):
    return np.ascontiguousarray(xs.reshape(NBLK, T, KC, 128).transpose(0, 3, 2, 1)).reshape(NBLK, 128, KC * T)


def _from_blocks(yb):
    return yb.reshape(NBLK, 128, KC, T).transpose(0, 3, 2, 1).reshape(TOK, D)


def kernel(**inputs):
    x = np.asarray(inputs["x"], np.float32)
    mem = np.asarray(inputs["mem"], np.float32)
    com = _common_inputs(inputs)
    cosT, sinT = _rope_tables()
    per_core = []
    for c in range(8):
        b, half = c // 2, c % 2
        d = dict(com)
        d["mem"] = np.ascontiguousarray(mem[b])
        d["cosT"] = np.ascontiguousarray(cosT[:, half * TOK:(half + 1) * TOK])
        d["sinT"] = np.ascontiguousarray(sinT[:, half * TOK:(half + 1) * TOK])
        per_core.append(d)
    out = np.empty((4, SEQ, D), np.float32)
    if FUSED:
        nc = _get_prog((0, 1, 2), True)
        in_maps = []
        for c in range(8):
            b, half = c // 2, c % 2
            d = dict(per_core[c])
            d["x_in"] = _to_blocks(x[b, half * TOK:(half + 1) * TOK])
            in_maps.append(d)
        res = run_bass_kernel_spmd(nc, in_maps, core_ids=list(range(8)))
        for c in range(8):
            b, half = c // 2, c % 2
            out[b, half * TOK:(half + 1) * TOK] = _from_blocks(np.asarray(res.results[c]["y_out"]))
        return out
    state = [dict() for _ in range(8)]
    for ph in range(3):
        nc = _get_prog((ph,), False)
        wsl = _slice_weights(com, (ph,))
        in_maps = []
        for c in range(8):
            b, half = c // 2, c % 2
            d = dict(per_core[c])
            d.update({k: wsl[k] for k in ("w_in_a", "w_in_b", "w_mem_kv", "w_out", "w_gate_up", "w_down")})
            if ph == 0:
                d["x_in"] = _to_blocks(x[b, half * TOK:(half + 1) * TOK])
            else:
                d["xpark_in"] = state[c]["xpark"]
                e = ph - 1
                p0, p1 = state[2 * b], state[2 * b + 1]
                d[f"kT_all{e}"] = np.concatenate([p0["kT"], p1["kT"]], axis=0)
                d[f"v_all{e}"] = np.concatenate([p0["v"], p1["v"]], axis=0)
            in_maps.append(d)
        res = run_bass_kernel_spmd(nc, in_maps, core_ids=list(range(8)))
        new_state = [dict() for _ in range(8)]
        for c in range(8):
            r = res.results[c]
            if ph < 2:
                new_state[c]["xpark"] = r["xpark_out"]
                new_state[c]["kT"] = r[f"kT_loc{ph}"]
                new_state[c]["v"] = r[f"v_loc{ph}"]
            else:
                b, half = c // 2, c % 2
                out[b, half * TOK:(half + 1) * TOK] = _from_blocks(np.asarray(r["y_out"]))
        state = new_state
    return out
```
